# Optimizing a Trainium2 kernel written in Bass

```python
import math
import jax, jax.numpy as jnp
from jax import lax
import numpy as np

D_MODEL = 1024
BATCH = 4
SEQ = 8192
DEPTH = 1

PLE_DIM = 256
EPS = 1e-6
BLOCK = 128
NEG = -1e30

A_HEADS = 16
A_KV_HEADS = 2
A_HEAD_DIM = 64
A_WIDTH = A_HEADS * A_HEAD_DIM
A_KV_WIDTH = A_KV_HEADS * A_HEAD_DIM
WINDOW = 128
N_BUCKETS = 32
MAX_DISTANCE = 128

B_HEADS = 16
B_Q_RANK = 256
B_KV_RANK = 128
B_NOPE = 64
B_ROPE = 32
B_QK_DIM = B_NOPE + B_ROPE
B_VDIM = 64
B_WIDTH = B_HEADS * B_VDIM
ROPE_THETA = 10000.0

SPLIT_SIZES = (A_WIDTH, A_KV_WIDTH, A_KV_WIDTH, A_WIDTH,
               B_Q_RANK, B_KV_RANK, B_ROPE, B_WIDTH,
               D_MODEL, D_MODEL)
D_IN = 2 * A_WIDTH + 2 * A_KV_WIDTH + B_Q_RANK + B_KV_RANK + B_ROPE + B_WIDTH + 2 * D_MODEL

kernel_name = "hybrid_swa_sink_mla_gated_block"


def _rms(x, g):
    x32 = x.astype(jnp.float32)
    y = x32 * lax.rsqrt(jnp.mean(x32 * x32, axis=-1, keepdims=True) + EPS)
    return (y * g.astype(jnp.float32)).astype(x.dtype)


def _split(z):
    offsets = [int(o) for o in np.cumsum(SPLIT_SIZES)[:-1]]
    return jnp.split(z, offsets, axis=-1)


def _t5_bucket(dist):
    max_exact = N_BUCKETS // 2
    d = jnp.maximum(dist, 1).astype(jnp.float32)
    large = max_exact + (jnp.log(d / max_exact) / math.log(MAX_DISTANCE / max_exact)
                         * (N_BUCKETS - max_exact)).astype(jnp.int32)
    large = jnp.minimum(large, N_BUCKETS - 1)
    return jnp.where(dist < max_exact, dist, large)


def _rope_tables(positions, dim):
    inv_freq = ROPE_THETA ** (-jnp.arange(0, dim, 2, dtype=jnp.float32) / dim)
    ang = positions.astype(jnp.float32)[..., None] * inv_freq
    return jnp.cos(ang), jnp.sin(ang)


def _rope(x, cos, sin):
    cos = cos.astype(x.dtype)
    sin = sin.astype(x.dtype)
    x1, x2 = jnp.split(x, 2, axis=-1)
    return jnp.concatenate([x1 * cos - x2 * sin, x2 * cos + x1 * sin], axis=-1)


def _swa_branch(q, k, v, rel_bias, sinks, qn_g, kn_g):
    b, s_len, _ = q.shape
    nb = s_len // BLOCK
    grp = A_HEADS // A_KV_HEADS
    q = _rms(q.reshape(b, nb, BLOCK, A_KV_HEADS, grp, A_HEAD_DIM), qn_g)
    k = _rms(k.reshape(b, nb, BLOCK, A_KV_HEADS, A_HEAD_DIM), kn_g)
    v = v.reshape(b, nb, BLOCK, A_KV_HEADS, A_HEAD_DIM)
    pad = jnp.zeros_like(k[:, :1])
    k_band = jnp.concatenate([jnp.concatenate([pad, k[:, :-1]], axis=1), k], axis=2)
    v_band = jnp.concatenate([jnp.concatenate([pad, v[:, :-1]], axis=1), v], axis=2)
    scale = A_HEAD_DIM ** -0.5
    s = jnp.einsum('bnqkgd,bnckd->bnkgqc', q, k_band).astype(jnp.float32) * scale
    dist = (jnp.arange(BLOCK)[:, None] + BLOCK) - jnp.arange(2 * BLOCK)[None, :]
    valid = (dist >= 0) & (dist < WINDOW)
    bias = rel_bias.astype(jnp.float32)[_t5_bucket(jnp.maximum(dist, 0))]
    bias = jnp.transpose(bias, (2, 0, 1)).reshape(A_KV_HEADS, grp, BLOCK, 2 * BLOCK)
    first_pad = (jnp.arange(nb)[:, None, None] == 0) & (jnp.arange(2 * BLOCK)[None, None, :] < BLOCK)
    mask = valid[None] & ~first_pad
    s = jnp.where(mask[None, :, None, None], s + bias, NEG)
    sink = sinks.astype(jnp.float32).reshape(1, 1, A_KV_HEADS, grp, 1, 1)
    m = jnp.maximum(jnp.max(s, axis=-1, keepdims=True), sink)
    e = jnp.exp(s - m)
    probs = e / (jnp.sum(e, axis=-1, keepdims=True) + jnp.exp(sink - m))
    o = jnp.einsum('bnkgqc,bnckd->bnqkgd', probs.astype(v.dtype), v_band)
    return o.reshape(b, s_len, A_WIDTH)


def _mla_branch(c_q, c_kv, k_r, positions, cq_g, w_uq, ckv_g, w_uk, w_uv, qn_g, kn_g, krn_g):
    b, s_len, _ = c_q.shape
    nb = s_len // BLOCK
    cos, sin = _rope_tables(positions, B_ROPE)
    q = (_rms(c_q, cq_g) @ w_uq).reshape(b, s_len, B_HEADS, B_QK_DIM)
    q = _rms(q, qn_g)
    q = jnp.concatenate([q[..., :B_NOPE], _rope(q[..., B_NOPE:], cos[:, :, None], sin[:, :, None])], axis=-1)
    ckv = _rms(c_kv, ckv_g)
    k_nope = _rms((ckv @ w_uk).reshape(b, s_len, B_HEADS, B_NOPE), kn_g)
    v = (ckv @ w_uv).reshape(b, s_len, B_HEADS, B_VDIM)
    k_rope = _rope(_rms(k_r, krn_g), cos, sin)
    scale = B_QK_DIM ** -0.5
    q_blocks = jnp.transpose(q.reshape(b, nb, BLOCK, B_HEADS, B_QK_DIM), (1, 0, 2, 3, 4))
    k_pos = jnp.arange(s_len)

    def one_block(args):
        q_blk, idx = args
        s = (jnp.einsum('bqhd,bkhd->bhqk', q_blk[..., :B_NOPE], k_nope)
             + jnp.einsum('bqhr,bkr->bhqk', q_blk[..., B_NOPE:], k_rope)).astype(jnp.float32) * scale
        q_pos = idx * BLOCK + jnp.arange(BLOCK)
        s = jnp.where(k_pos[None, :] <= q_pos[:, None], s, NEG)
        probs = jax.nn.softmax(s, axis=-1)
        return jnp.einsum('bhqk,bkhd->bqhd', probs.astype(v.dtype), v)

    o = lax.map(one_block, (q_blocks, jnp.arange(nb)))
    return jnp.transpose(o, (1, 0, 2, 3, 4)).reshape(b, s_len, B_WIDTH)


def setup_inputs(seed: int = 0) -> dict:
    key = jax.random.key(seed)
    ks = jax.random.split(key, 24)
    f32 = jnp.float32

    def w(k, shape, fan_in):
        return jax.random.normal(k, shape, f32) * fan_in ** -0.5

    def gain(k, shape):
        return 1.0 + 0.1 * jax.random.normal(k, shape, f32)

    return {
        "x": jax.random.normal(ks[0], (BATCH, SEQ, D_MODEL), f32),
        "p": jax.random.normal(ks[1], (DEPTH, BATCH, SEQ, PLE_DIM), f32),
        "positions": jnp.broadcast_to(jnp.arange(SEQ, dtype=jnp.int32)[None], (BATCH, SEQ)),
        "norm_g": gain(ks[2], (DEPTH, D_MODEL)),
        "w_in": w(ks[3], (DEPTH, D_MODEL, D_IN), D_MODEL),
        "a_q_norm": gain(ks[4], (DEPTH, A_HEAD_DIM)),
        "a_k_norm": gain(ks[5], (DEPTH, A_HEAD_DIM)),
        "a_sinks": 0.5 * jax.random.normal(ks[6], (DEPTH, A_HEADS), f32),
        "rel_bias": 0.5 * jax.random.normal(ks[7], (N_BUCKETS, A_HEADS), f32),
        "w_o_a": w(ks[8], (DEPTH, A_WIDTH, D_MODEL), A_WIDTH),
        "b_cq_norm": gain(ks[9], (DEPTH, B_Q_RANK)),
        "w_uq": w(ks[10], (DEPTH, B_Q_RANK, B_HEADS * B_QK_DIM), B_Q_RANK),
        "b_ckv_norm": gain(ks[11], (DEPTH, B_KV_RANK)),
        "w_uk": w(ks[12], (DEPTH, B_KV_RANK, B_HEADS * B_NOPE), B_KV_RANK),
        "w_uv": w(ks[13], (DEPTH, B_KV_RANK, B_HEADS * B_VDIM), B_KV_RANK),
        "b_q_norm": gain(ks[14], (DEPTH, B_QK_DIM)),
        "b_k_norm": gain(ks[15], (DEPTH, B_NOPE)),
        "b_kr_norm": gain(ks[16], (DEPTH, B_ROPE)),
        "w_o_b": w(ks[17], (DEPTH, B_WIDTH, D_MODEL), B_WIDTH),
        "w_out": w(ks[18], (DEPTH, D_MODEL, D_MODEL), D_MODEL),
        "ple_norm_g": gain(ks[19], (DEPTH, D_MODEL)),
        "w_ple_gate": w(ks[20], (DEPTH, D_MODEL, D_MODEL), D_MODEL),
        "w_ple_proj": w(ks[21], (DEPTH, PLE_DIM, D_MODEL), PLE_DIM),
        "ple_post_g": gain(ks[22], (DEPTH, D_MODEL)),
    }


def reference(x, p, positions, norm_g, w_in, a_q_norm, a_k_norm, a_sinks, rel_bias, w_o_a,
              b_cq_norm, w_uq, b_ckv_norm, w_uk, w_uv, b_q_norm, b_k_norm, b_kr_norm, w_o_b,
              w_out, ple_norm_g, w_ple_gate, w_ple_proj, ple_post_g):
    for i in range(DEPTH):
        h = _rms(x, norm_g[i])
        z = h @ w_in[i]
        q_a, k_a, v_a, z_a, c_q, c_kv, k_r, z_b, g_a, g_b = _split(z)
        y_a = _swa_branch(q_a, k_a, v_a, rel_bias, a_sinks[i], a_q_norm[i], a_k_norm[i])
        y_a = (y_a * jax.nn.silu(z_a)) @ w_o_a[i]
        y_b = _mla_branch(c_q, c_kv, k_r, positions, b_cq_norm[i], w_uq[i], b_ckv_norm[i],
                          w_uk[i], w_uv[i], b_q_norm[i], b_k_norm[i], b_kr_norm[i])
        y_b = (y_b * jax.nn.silu(z_b)) @ w_o_b[i]
        merged = jax.nn.sigmoid(g_a) * y_a + jax.nn.sigmoid(g_b) * y_b
        x = x + merged @ w_out[i]
        gate = jax.nn.sigmoid(_rms(x, ple_norm_g[i]) @ w_ple_gate[i])
        emb = _rms(p[i] @ w_ple_proj[i], ple_post_g[i])
        x = x + gate * emb
    return x
```

```python
import math
from contextlib import ExitStack
import numpy as np
import ml_dtypes
import concourse.bass as bass
import concourse.mybir as mybir
from concourse.bass_utils import run_bass_kernel_spmd

F32 = mybir.dt.float32
BF16 = mybir.dt.bfloat16
I32 = mybir.dt.int32
AF = mybir.ActivationFunctionType
ALU = mybir.AluOpType

DEBUG = False
EPS = 1e-6
C_GN, C_GPLE, C_GPOST, C_GAQ, C_GAK, C_GCQ, C_GCKV, C_GBQ, C_GBQS, C_GBK, C_GKR, C_GKRS = 0, 8, 16, 24, 25, 26, 28, 29, 30, 31, 32, 33
C_SINK, C_INVF, C_SSGN, C_EPS, C_HPI = 34, 50, 51, 52, 53
TWO_PI = 2.0 * math.pi
CW1 = 6.28125
_r = np.float32(TWO_PI - 6.28125)
CW2 = float(np.frombuffer(np.uint32(np.frombuffer(_r.tobytes(), np.uint32)[0] & np.uint32(0xFFFFF000)).tobytes(), np.float32)[0])
CW3 = float(TWO_PI - 6.28125 - CW2)
MAGIC = 12582912.0
PI_IN = 3.1415925


PHASES = {"2A", "2B", "1", "3", "4"}
DEBUG_KEEP = set()
LIM = {"a_chunks": 16, "a_stage": 9, "a_setup": 1}


def _phase(name):
    if name in PHASES:
        with ExitStack() as ph:
            yield ph


def run_pipe(bodies, stagger):
    active = []
    nxt = 0
    step = 0
    start_step = 0
    while nxt < len(bodies) or active:
        if nxt < len(bodies) and len(active) < 2 and (not active or step >= start_step + stagger):
            active.append(bodies[nxt]())
            nxt += 1
            start_step = step
        for g_ in list(active):
            try:
                next(g_)
            except StopIteration:
                active.remove(g_)
        step += 1


def wave(gens):
    gens = list(gens)
    while gens:
        for g_ in list(gens):
            try:
                next(g_)
            except StopIteration:
                gens.remove(g_)


class Trk:
    __slots__ = ("w", "r", "name")

    def __init__(self, name=""):
        self.w = None
        self.r = {}
        self.name = name


class RR:
    def __init__(self, items):
        self.items = items
        self.i = 0

    def next(self):
        it = self.items[self.i % len(self.items)]
        self.i += 1
        return it


class KB:
    ENG = ("pe", "act", "dve", "pool", "sp")

    def __init__(self, nc, ndma=20, nsw=8):
        self.nc = nc
        self.e = {"pe": nc.tensor, "act": nc.scalar, "dve": nc.vector, "pool": nc.gpsimd, "sp": nc.sync}
        self.sem = {k: nc.alloc_semaphore(name="sem_" + k) for k in self.ENG}
        self.cnt = {k: 0 for k in self.ENG}
        self.seen = {k: {} for k in self.ENG}
        self.dsem = {"sp": [nc.alloc_semaphore(name="dsem%d" % i) for i in range(ndma)],
                     "pool": [nc.alloc_semaphore(name="wsem%d" % i) for i in range(nsw)],
                     "act": [nc.alloc_semaphore(name="asem%d" % i) for i in range(6)]}
        self.dval = {"sp": [0] * ndma, "pool": [0] * nsw, "act": [0] * 6}
        self.dnext = {"sp": 0, "pool": 0, "act": 0}
        self.pend = {k: [] for k in self.ENG}

    def _need(self, e, key, val):
        if self.seen[e].get(key, 0) >= val:
            return
        sem = self.sem[key] if isinstance(key, str) else self.dsem[key[1]][key[2]]
        self.e[e].wait_ge(sem, val)
        self.seen[e][key] = val

    def _dep1(self, e, key, val):
        if key == "pe" and e == "pe":
            return
        self._need(e, key, val)

    def _deps(self, e, reads, writes):
        for t in reads:
            if t.w is not None:
                self._dep1(e, t.w[0], t.w[1])
        for t in writes:
            if t.w is not None:
                self._dep1(e, t.w[0], t.w[1])
            for k, v in t.r.items():
                self._dep1(e, k, v)

    def op(self, e, fn, reads=(), writes=(), sig=True):
        self._deps(e, reads, writes)
        inst = fn(self.e[e])
        if sig:
            self.cnt[e] += 1
            idx = self.cnt[e]
            inst.then_inc(self.sem[e], 1)
            for (rr, ww) in self.pend[e] + [(reads, writes)]:
                for t in ww:
                    t.w = (e, idx)
                    t.r = {}
                for t in rr:
                    if t.r.get(e, 0) < idx:
                        t.r[e] = idx
            self.pend[e] = []
        else:
            self.pend[e].append((tuple(reads), tuple(writes)))
        return inst

    def dma(self, q, out, in_, reads=(), writes=(), **kw):
        i = self.dnext[q]
        self.dnext[q] = (i + 1) % len(self.dsem[q])
        key = ("d", q, i)
        if self.dval[q][i] > 0:
            self._need(q, key, self.dval[q][i])
        self._deps(q, reads, writes)
        inst = self.e[q].dma_start(out=out, in_=in_, **kw)
        self.dval[q][i] += 16
        inst.then_inc(self.dsem[q][i], 16)
        for t in writes:
            t.w = (key, self.dval[q][i])
            t.r = {}
        for t in reads:
            t.r[key] = self.dval[q][i]

    def _alld(self):
        for q in ("sp", "pool", "act"):
            for i in range(len(self.dsem[q])):
                if self.dval[q][i] > 0:
                    yield ("d", q, i), self.dval[q][i]

    def barrier(self):
        for e in self.ENG:
            assert not self.pend[e]
        for e in self.ENG:
            for f in self.ENG:
                if f != e and self.cnt[f] > 0:
                    self._need(e, f, self.cnt[f])
            for key, val in self._alld():
                self._need(e, key, val)

    def finish(self):
        for key, val in self._alld():
            self._need("sp", key, val)


def build(debug=False):
    nc = bass.Bass("TRN2", target_bir_lowering=False)
    kb = KB(nc)

    def din(name, shape, dt=F32):
        return nc.dram_tensor(name, list(shape), dt, kind="ExternalInput").ap()

    def dscr(name, shape, dt):
        return nc.dram_tensor(name, list(shape), dt, kind=("ExternalOutput" if (debug and name in DEBUG_KEEP) else "Internal")).ap()

    xbT = din("xbT", [1024, 8192]); xsT = din("xsT", [1024, 6144]); pT = din("pT", [256, 4096])
    posb = din("posb", [32, 8192], I32); poso = din("poso", [32, 4096], I32)
    hfl = din("hfl", [16, 128, 128])
    cst = din("cst", [128, 64]); cmat = din("cmat", [128, 640]); bsel = din("bsel", [32, 128]); relb = din("relb", [32, 16])
    cmk = din("cmk", [128, 4096])
    w2a = din("w2a", [1024, 3456]); woa = din("woa", [1024, 1024]); w2b = din("w2b", [1024, 256])
    wuq = din("wuq", [256, 1536]); wuqs = din("wuqs", [256, 1536]); w1 = din("w1", [1024, 320])
    wuk = din("wuk", [128, 1024]); wuv = din("wuv", [128, 1024])
    w4 = din("w4", [1024, 2048]); wob = din("wob", [1024, 1024]); wout = din("wout", [1024, 1024])
    wpg = din("wpg", [1024, 1024]); wpp = din("wpp", [256, 1024])
    outT = nc.dram_tensor("outT", [1024, 4096], F32, kind="ExternalOutput").ap()
    MAd = dscr("MAd", [1024, 4096], F32)
    Qd = dscr("Qd", [16, 96, 4096], BF16)
    Kd = dscr("Kd", [1024, 8192], BF16)
    Vd = dscr("Vd", [16, 16, 128, 512], BF16)
    OBd = dscr("OBd", [1024, 4096], F32)
    EBv = dscr("EBv", [16, 384], F32)

    def wview(w):
        return w.rearrange("(c p) n -> p c n", p=128)

    with ExitStack() as g:
        def sb(name, shape, dt, es=g):
            return es.enter_context(nc.sbuf_tensor(name, list(shape), dt))

        def pst(name, shape, es):
            return es.enter_context(nc.psum_tensor(name, list(shape), F32))

        POSIg = [(sb("POSIg%d" % i, [96, 512], I32), Trk()) for i in range(2)]
        CT = sb("CT", [128, 64], F32); CTt = Trk()
        CMAT = sb("CMAT", [128, 640], BF16); CMATt = Trk()
        kb.dma("sp", CT[:, :], cst[:, :], writes=[CTt])
        kb.dma("pool", CMAT[:, :], cmat[:, :], writes=[CMATt])
        ONES = CMAT[:, 0:128]; BLK64 = CMAT[:, 128:256]; ONE96 = CMAT[0:96, 256:352]; KR32 = CMAT[0:96, 384:480]; JFLIP = CMAT[:, 512:640]

        def gn(col, rows=slice(0, 128)):
            return CT[rows, col:col + 1]

        def norm1024(X, Xt, H, Htl, nT, SQ, SQt, RS, RSt, gcol0, nb):
            kb.op("act", lambda e: e.activation(out=SQ[:, :, 0:nT], in_=X[:, :, 0:nT], func=AF.Square), reads=[Xt, CTt], writes=[SQt])
            ps, pt = nb()
            for c in range(8):
                kb.op("pe", lambda e: e.matmul(ps[:, 0:nT], lhsT=ONES, rhs=SQ[:, c, 0:nT], start=(c == 0), stop=(c == 7)),
                      reads=[SQt, CMATt], writes=[pt], sig=(c == 7))
            kb.op("act", lambda e: e.activation(out=RS[:, 0:nT], in_=ps[:, 0:nT], func=AF.Ln, scale=1.0 / 1024, bias=gn(C_EPS)),
                  reads=[pt, CTt], writes=[RSt])
            kb.op("act", lambda e: e.activation(out=RS[:, 0:nT], in_=RS[:, 0:nT], func=AF.Exp, scale=-0.5), reads=[RSt], writes=[RSt])
            for c in range(8):
                kb.op("dve", lambda e: e.scalar_tensor_tensor(out=H[:, c, 0:nT], in0=X[:, c, 0:nT], scalar=gn(gcol0 + c), in1=RS[:, 0:nT],
                                                              op0=ALU.mult, op1=ALU.mult), reads=[Xt, RSt, CTt], writes=[Htl[c]])

        def rstd_from(z, zt, rows, n, lhsT, inv_n, sqp, rsp, nb):
            sq, sqt = sqp.next()
            kb.op("act", lambda e: e.activation(out=sq[rows, 0:n], in_=z[rows, 0:n], func=AF.Square), reads=[zt], writes=[sqt])
            ps2, p2t = nb()
            kb.op("pe", lambda e: e.matmul(ps2[rows, 0:n], lhsT=lhsT, rhs=sq[rows, 0:n], start=True, stop=True), reads=[sqt, CMATt], writes=[p2t])
            rs, rst = rsp.next()
            kb.op("act", lambda e: e.activation(out=rs[rows, 0:n], in_=ps2[rows, 0:n], func=AF.Ln, scale=inv_n, bias=gn(C_EPS, rows)),
                  reads=[p2t, CTt], writes=[rst])
            kb.op("act", lambda e: e.activation(out=rs[rows, 0:n], in_=rs[rows, 0:n], func=AF.Exp, scale=-0.5), reads=[rst], writes=[rst])
            return rs, rst

        def rstd_gen(z, zt, rows, n, lhsT, inv_n, sqp, rsp, nb, out):
            sq, sqt = sqp.next(); ps2, p2t = nb(); rs, rst = rsp.next()
            kb.op("act", lambda e: e.activation(out=sq[rows, 0:n], in_=z[rows, 0:n], func=AF.Square), reads=[zt], writes=[sqt])
            yield
            kb.op("pe", lambda e: e.matmul(ps2[rows, 0:n], lhsT=lhsT, rhs=sq[rows, 0:n], start=True, stop=True), reads=[sqt, CMATt], writes=[p2t])
            yield
            kb.op("act", lambda e: e.activation(out=rs[rows, 0:n], in_=ps2[rows, 0:n], func=AF.Ln, scale=inv_n, bias=gn(C_EPS, rows)),
                  reads=[p2t, CTt], writes=[rst])
            yield
            kb.op("act", lambda e: e.activation(out=rs[rows, 0:n], in_=rs[rows, 0:n], func=AF.Exp, scale=-0.5), reads=[rst], writes=[rst])
            yield
            out.append((rs, rst))

        def rope_tables(POSI, POSIt, n, ANG, KK, SN, CO, tt):
            R = slice(64, 96)
            kb.op("dve", lambda e: e.tensor_copy(out=ANG[R, 0:n], in_=POSI[R, 0:n]), reads=[POSIt], writes=[tt["ang"]])
            kb.op("dve", lambda e: e.tensor_scalar(out=ANG[R, 0:n], in0=ANG[R, 0:n], scalar1=gn(C_INVF, R), scalar2=None, op0=ALU.mult),
                  reads=[tt["ang"], CTt], writes=[tt["ang"]])
            kb.op("dve", lambda e: e.tensor_scalar(out=KK[R, 0:n], in0=ANG[R, 0:n], scalar1=1.0 / TWO_PI, scalar2=MAGIC, op0=ALU.mult, op1=ALU.add),
                  reads=[tt["ang"]], writes=[tt["kk"]])
            kb.op("dve", lambda e: e.tensor_scalar(out=KK[R, 0:n], in0=KK[R, 0:n], scalar1=-MAGIC, scalar2=None, op0=ALU.add),
                  reads=[tt["kk"]], writes=[tt["kk"]])
            for cw in (CW1, CW2, CW3):
                kb.op("dve", lambda e: e.scalar_tensor_tensor(out=ANG[R, 0:n], in0=KK[R, 0:n], scalar=-cw, in1=ANG[R, 0:n], op0=ALU.mult, op1=ALU.add),
                      reads=[tt["kk"], tt["ang"]], writes=[tt["ang"]])
            kb.op("dve", lambda e: e.tensor_scalar(out=KK[R, 0:n], in0=ANG[R, 0:n], scalar1=PI_IN, scalar2=None, op0=ALU.is_gt),
                  reads=[tt["ang"], tt["kk"]], writes=[tt["kk"]])
            kb.op("dve", lambda e: e.scalar_tensor_tensor(out=ANG[R, 0:n], in0=KK[R, 0:n], scalar=-TWO_PI, in1=ANG[R, 0:n], op0=ALU.mult, op1=ALU.add),
                  reads=[tt["kk"], tt["ang"]], writes=[tt["ang"]])
            kb.op("dve", lambda e: e.tensor_scalar(out=KK[R, 0:n], in0=ANG[R, 0:n], scalar1=-PI_IN, scalar2=None, op0=ALU.is_lt),
                  reads=[tt["ang"], tt["kk"]], writes=[tt["kk"]])
            kb.op("dve", lambda e: e.scalar_tensor_tensor(out=ANG[R, 0:n], in0=KK[R, 0:n], scalar=TWO_PI, in1=ANG[R, 0:n], op0=ALU.mult, op1=ALU.add),
                  reads=[tt["kk"], tt["ang"]], writes=[tt["ang"]])
            kb.op("dve", lambda e: e.tensor_scalar(out=ANG[R, 0:n], in0=ANG[R, 0:n], scalar1=PI_IN, scalar2=-PI_IN, op0=ALU.min, op1=ALU.max),
                  reads=[tt["ang"]], writes=[tt["ang"]])
            kb.op("act", lambda e: e.activation(out=SN[R, 0:n], in_=ANG[R, 0:n], func=AF.Sin), reads=[tt["ang"]], writes=[tt["sn"]])
            kb.op("dve", lambda e: e.tensor_scalar(out=SN[R, 0:n], in0=SN[R, 0:n], scalar1=gn(C_SSGN, R), scalar2=None, op0=ALU.mult),
                  reads=[tt["sn"], CTt], writes=[tt["sn"]])
            kb.op("act", lambda e: e.activation(out=KK[R, 0:n], in_=ANG[R, 0:n], func=AF.Abs),
                  reads=[tt["ang"]], writes=[tt["kk"]])
            kb.op("act", lambda e: e.activation(out=CO[R, 0:n], in_=KK[R, 0:n], func=AF.Sin, scale=-1.0, bias=gn(C_HPI, R)),
                  reads=[tt["kk"], CTt], writes=[tt["co"]])

        for ph in _phase("2A"):
            def s2(name, shape, dt):
                return sb("a_" + name, shape, dt, ph)
            W2A = s2("W2A", [128, 8, 3456], BF16); W2At = Trk()
            WOA = s2("WOA", [128, 8, 1024], BF16); WOAt = Trk()
            kb.dma("pool", W2A[:, :, 0:1408], wview(w2a)[:, :, 0:1408], writes=[W2At])
            kb.dma("pool", W2A[:, :, 1408:3456], wview(w2a)[:, :, 1408:3456], writes=[W2At])
            kb.dma("pool", WOA[:, :, :], wview(woa), writes=[WOAt])
            EB = s2("EB", [128, 16, 2, 128], BF16); EBt = Trk()
            SKROW = s2("SKROW", [1, 16, 2, 128], BF16); skt = Trk()
            ESKR = s2("ESKR", [1, 16], F32); eskt = Trk()
            ONER = s2("ONER", [1, 128], BF16); onert = Trk()
            with ExitStack() as st:
                def s3(name, shape, dt):
                    return sb("a_" + name, shape, dt, st)
                BSEL = s3("BSEL", [32, 128], F32); RELB = s3("RELB", [32, 16], F32); bt_ = Trk(); rt_ = Trk()
                kb.dma("sp", BSEL[:, :], bsel[:, :], writes=[bt_]); kb.dma("sp", RELB[:, :], relb[:, :], writes=[rt_])
                ZR = s3("ZR", [16, 384], F32); zrt = Trk()
                kb.op("dve", lambda e: e.memset(ZR[:, :], 0.0), writes=[zrt])
                EBt_d = Trk()
                kb.dma("sp", EBv[:, :], ZR[:, :], reads=[zrt], writes=[EBt_d])
                EFT = s3("EFT", [16, 128], F32); eft = Trk()
                with ExitStack() as pp:
                    psb = pst("a_psb", [128, 512], pp); psbt = Trk()
                    kb.op("pe", lambda e: e.matmul(psb[0:16, 0:128], lhsT=RELB[:, :], rhs=BSEL[:, :], start=True, stop=True), reads=[bt_, rt_], writes=[psbt])
                    kb.op("act", lambda e: e.activation(out=EFT[:, :], in_=psb[0:16, 0:128], func=AF.Exp), reads=[psbt], writes=[eft])
                kb.dma("sp", EBv[:, 127:255], EFT[:, :], reads=[eft], writes=[EBt_d])
                EBf = s3("EBf", [128, 16, 2, 128], BF16); EBft = Trk()
                for pv in range(2):
                    src = bass.AP(tensor=EBv.tensor, offset=(128 if pv == 0 else 0), ap=[[1, 128], [384, 16], [1, 128]])
                    kb.dma("pool", EBf[:, :, pv, :], src, reads=[EBt_d], writes=[EBft])
                with ExitStack() as pp:
                    psj = [(pst("a_psj%d" % i, [128, 512], pp), Trk()) for i in range(2)]
                    EBfl = EBf[:, :, :, :].rearrange("p h v q -> p (h v q)")
                    EBl = EB[:, :, :, :].rearrange("p h v q -> p (h v q)")
                    for i in range(8):
                        pj, pjt = psj[i % 2]
                        kb.op("pe", lambda e: e.matmul(pj[:, :], lhsT=JFLIP, rhs=EBfl[:, i * 512:(i + 1) * 512], start=True, stop=True), reads=[EBft, CMATt], writes=[pjt])
                        kb.op("dve", lambda e: e.tensor_copy(out=EBl[:, i * 512:(i + 1) * 512], in_=pj[:, :]), reads=[pjt], writes=[EBt])
                kb.barrier()
            kb.op("act", lambda e: e.activation(out=ESKR[0:1, :], in_=CT[0:1, C_SINK:C_SINK + 16], func=AF.Exp), reads=[CTt], writes=[eskt])
            kb.op("dve", lambda e: e.memset(SKROW[:, :, :, :], 0.0), writes=[skt])
            kb.op("dve", lambda e: e.memset(ONER[:, :], 1.0), writes=[onert])
            for h in range(16):
                for par in range(2):
                    sc = slice(64, 128) if par == 0 else slice(0, 64)
                    kb.op("dve", lambda e: e.tensor_scalar(out=SKROW[0:1, h, par, sc], in0=SKROW[0:1, h, par, sc], scalar1=ESKR[0:1, h:h + 1],
                                                           scalar2=None, op0=ALU.add), reads=[skt, eskt], writes=[skt])
            X = [s2("X%d" % i, [128, 8, 384], F32) for i in range(2)]; Xt = [Trk(), Trk()]
            SQ = s2("SQ", [128, 8, 384], BF16); SQt = Trk()
            RSs = [(s2("RSb%d" % i, [128, 384], F32), Trk()) for i in range(2)]
            Hs = [(s2("H%d" % i, [128, 8, 384], BF16), [Trk() for _ in range(8)]) for i in range(2)]
            KANs = [([s2("KAN%d_%d" % (k, i), [128, 384], BF16) for i in range(2)], [Trk(), Trk()]) for k in range(2)]
            VAs = [(s2("VA%d" % i, [128, 3, 2, 2, 128], BF16), Trk(), Trk()) for i in range(2)]
            QANs = [(s2("QAN%d" % i, [128, 8, 256], BF16), [Trk() for _ in range(8)]) for i in range(2)]
            SZAs = [(s2("SZA%d" % i, [128, 8, 256], F32), [Trk() for _ in range(8)]) for i in range(2)]
            YAs = [(s2("YA%d" % i, [128, 8, 256], BF16), [Trk() for _ in range(8)]) for i in range(2)]
            sqp = RR([(s2("sq%d" % i, [128, 384], BF16), Trk()) for i in range(4)])
            rsp = RR([(s2("rs%d" % i, [128, 384], F32), Trk()) for i in range(4)])
            Ep = RR([(s2("E%d" % i, [128, 4, 2, 128], BF16), Trk()) for i in range(3)])
            DENp = RR([(s2("DEN%d" % i, [128, 2, 128], F32), Trk()) for i in range(4)])
            TMPp = RR([(s2("TMP%d" % i, [128, 2, 128], F32), Trk()) for i in range(4)])
            SGp = RR([(s2("SG%d" % i, [128, 256], F32), Trk()) for i in range(2)])
            MAp = RR([(s2("MAo%d" % i, [128, 256], F32), Trk()) for i in range(3)])
            PSg = RR([(pst("a_ps%d" % i, [128, 512], ph), Trk()) for i in range(4)])
            PSs = RR([(pst("a_pss%d" % i, [128, 4, 2, 128], ph), Trk()) for i in range(2)])
            nb = PSg.next
            for i in range(2):
                kb.op("pool", lambda e: e.memset(VAs[i][0][:, :, :, :, :], 1.0), writes=[VAs[i][1], VAs[i][2]])
            xs3 = wview(xsT)

            def body_a(ch):
                Xc, Xct = X[ch % 2], Xt[ch % 2]
                RS, RSt = RSs[ch % 2]; H, Ht = Hs[ch % 2]; KAN, KANt = KANs[ch % 2]; VA, VAt, VAf = VAs[ch % 2]
                QAN, QANt = QANs[ch % 2]; SZA, SZAt = SZAs[ch % 2]; YA, YAt = YAs[ch % 2]
                if ch == 0:
                    kb.dma("act", X[0][:, :, :], xs3[:, :, 0:384], writes=[Xt[0]])
                if ch + 1 < LIM["a_chunks"]:
                    kb.dma("act", X[(ch + 1) % 2][:, :, :], xs3[:, :, (ch + 1) * 384:(ch + 2) * 384], writes=[Xt[(ch + 1) % 2]])
                norm1024(Xc, Xct, H, Ht, 384, SQ, SQt, RS, RSt, C_GN, nb)
                yield
                if LIM["a_stage"] < 2:
                    return
                kb.dma("pool", VA[:, 0, :, 0, 64:128], hfl[ch].rearrange("t (g e) -> t g e", g=2), reads=[], writes=[VAf])
                kb.dma("pool", VA[:, 0, :, 1, 0:64], hfl[ch].rearrange("t (g e) -> t g e", g=2), reads=[], writes=[VAf])
                def ch_ka(gk):
                    ps, pt = nb()
                    for c in range(8):
                        kb.op("pe", lambda e: e.matmul(ps[:, 0:384], lhsT=W2A[:, c, 1024 + gk * 128:1152 + gk * 128], rhs=H[:, c, :], start=(c == 0), stop=(c == 7)),
                              reads=[W2At, Ht[c]], writes=[pt], sig=(c == 7))
                    yield
                    o = []
                    yield from rstd_gen(ps, pt, slice(0, 128), 384, BLK64, 1.0 / 64, sqp, rsp, nb, o)
                    rs, rst = o[0]
                    kb.op("dve", lambda e: e.scalar_tensor_tensor(out=KAN[gk][:, :], in0=ps[:, 0:384], scalar=gn(C_GAK), in1=rs[:, 0:384], op0=ALU.mult, op1=ALU.mult),
                          reads=[pt, rst, CTt], writes=[KANt[gk]])
                wave([ch_ka(0), ch_ka(1)])
                yield
                if LIM["a_stage"] < 3:
                    return

                def ch_va(blk):
                    ps, pt = nb()
                    for c in range(8):
                        kb.op("pe", lambda e: e.matmul(ps[:, 0:128], lhsT=H[:, c, blk * 128:(blk + 1) * 128], rhs=W2A[:, c, 1280:1408], start=(c == 0), stop=(c == 7)),
                              reads=[W2At, Ht[c]], writes=[pt], sig=(c == 7))
                    yield
                    pv3 = ps[:, 0:128].rearrange("p (g e) -> p g e", g=2)
                    kb.op("dve", lambda e: e.tensor_copy(out=VA[:, blk, :, 0, 0:64], in_=pv3), reads=[pt], writes=[VAt])
                    yield
                    kb.op("dve", lambda e: e.tensor_copy(out=VA[:, blk, :, 1, 64:128], in_=pv3), reads=[pt], writes=[VAt])
                wave([ch_va(blk) for blk in range(3)])
                yield
                if LIM["a_stage"] < 4:
                    return

                def ch_qa(cq):
                    ps, pt = nb()
                    for c in range(8):
                        kb.op("pe", lambda e: e.matmul(ps[:, 0:256], lhsT=W2A[:, c, cq * 128:(cq + 1) * 128], rhs=H[:, c, 128:384], start=(c == 0), stop=(c == 7)),
                              reads=[W2At, Ht[c]], writes=[pt], sig=(c == 7))
                    yield
                    o = []
                    yield from rstd_gen(ps, pt, slice(0, 128), 256, BLK64, 1.0 / 64, sqp, rsp, nb, o)
                    rs, rst = o[0]
                    kb.op("dve", lambda e: e.scalar_tensor_tensor(out=QAN[:, cq, :], in0=ps[:, 0:256], scalar=gn(C_GAQ), in1=rs[:, 0:256], op0=ALU.mult, op1=ALU.mult),
                          reads=[pt, rst, CTt], writes=[QANt[cq]])
                for w in range(4):
                    wave([ch_qa(2 * w), ch_qa(2 * w + 1)])
                    yield

                def ch_za(cq):
                    ps, pt = nb()
                    for c in range(8):
                        kb.op("pe", lambda e: e.matmul(ps[:, 0:256], lhsT=W2A[:, c, 1408 + cq * 128:1536 + cq * 128], rhs=H[:, c, 128:384], start=(c == 0), stop=(c == 7)),
                              reads=[W2At, Ht[c]], writes=[pt], sig=(c == 7))
                    yield
                    kb.op("act", lambda e: e.activation(out=SZA[:, cq, :], in_=ps[:, 0:256], func=AF.Silu), reads=[pt], writes=[SZAt[cq]])
                for w in range(2):
                    wave([ch_za(cq) for cq in range(4 * w, 4 * w + 4)])
                yield
                if LIM["a_stage"] < 5:
                    return

                def ch_at(ob, hg):
                    qs = slice((ob - 1) * 128, ob * 128)
                    S, St = PSs.next()
                    O, Ot = nb()
                    E, Et = Ep.next()
                    gk = hg // 2
                    for hh in range(4):
                        h = hg * 4 + hh; par = h % 2; cq = h // 2
                        pr = slice(par * 64, par * 64 + 64)
                        for pv in range(2):
                            kblk = ob - 1 + pv
                            last = (hh == 3 and pv == 1)
                            kb.op("pe", lambda e: e.matmul(S[:, par * 2 + hh // 2, pv, :], lhsT=KAN[gk][pr, kblk * 128:(kblk + 1) * 128], rhs=QAN[pr, cq, qs], start=True, stop=True),
                                  reads=[KANt[gk], QANt[cq]], writes=[St], sig=last)
                    yield
                    kb.op("act", lambda e: e.activation(out=E[:, :, :, :], in_=S[:, :, :, :], func=AF.Exp, scale=0.125), reads=[St], writes=[Et])
                    yield
                    kb.op("dve", lambda e: e.tensor_tensor(out=E[:, :, :, :], in0=E[:, :, :, :], in1=EB[:, hg * 4:hg * 4 + 4, :, :], op=ALU.mult),
                          reads=[Et, EBt], writes=[Et])
                    yield
                    for hh in range(4):
                        h = hg * 4 + hh; par = h % 2
                        kb.op("pe", lambda e: e.matmul(O[:, hh * 128:(hh + 1) * 128], lhsT=SKROW[0:1, h, par, :], rhs=ONER[0:1, :], start=True, stop=False),
                              reads=[skt, onert], writes=[Ot], sig=False)
                        for pv in range(2):
                            kblk = ob - 1 + pv
                            kb.op("pe", lambda e: e.matmul(O[:, hh * 128:(hh + 1) * 128], lhsT=VA[:, kblk, gk, par, :], rhs=E[:, par * 2 + hh // 2, pv, :], start=False, stop=(pv == 1)),
                                  reads=[VAt, VAf, Et], writes=[Ot], sig=(hh == 3 and pv == 1))
                    yield
                    O3 = O[:, :].rearrange("p (h q) -> p h q", h=4)
                    c0 = hg * 2
                    dd = []
                    for par in range(2):
                        osl = slice(par * 64, par * 64 + 64); ssl = slice((1 - par) * 64, (1 - par) * 64 + 64)
                        DEN, DENt = DENp.next(); TMP, TMPt = TMPp.next()
                        dd.append((osl, ssl, DEN, DENt, TMP, TMPt))
                    for par in range(2):
                        osl, ssl, DEN, DENt, TMP, TMPt = dd[par]
                        kb.op("dve", lambda e: e.reciprocal(out=DEN[osl, :, :], in_=O3[ssl, par::2, :]), reads=[Ot], writes=[DENt])
                        yield
                    for par in range(2):
                        osl, ssl, DEN, DENt, TMP, TMPt = dd[par]
                        kb.op("dve", lambda e: e.tensor_tensor(out=TMP[osl, :, :], in0=O3[osl, par::2, :], in1=DEN[osl, :, :], op=ALU.mult),
                              reads=[Ot, DENt], writes=[TMPt])
                        yield
                    for par in range(2):
                        osl, ssl, DEN, DENt, TMP, TMPt = dd[par]
                        kb.op("dve", lambda e: e.tensor_tensor(out=YA[osl, c0:c0 + 2, qs], in0=TMP[osl, :, :], in1=SZA[osl, c0:c0 + 2, qs], op=ALU.mult),
                              reads=[TMPt, SZAt[c0], SZAt[c0 + 1]], writes=[YAt[c0], YAt[c0 + 1]])
                        yield
                for ob in (1, 2):
                    for w in range(2):
                        wave([ch_at(ob, 2 * w), ch_at(ob, 2 * w + 1)])
                        yield
                if LIM["a_stage"] < 6:
                    return

                def ch_out(oc):
                    py, pyt = nb(); pg, pgt = nb()
                    for c in range(8):
                        kb.op("pe", lambda e: e.matmul(py[:, 0:256], lhsT=WOA[:, c, oc * 128:(oc + 1) * 128], rhs=YA[:, c, :], start=(c == 0), stop=(c == 7)),
                              reads=[WOAt, YAt[c]], writes=[pyt], sig=(c == 7))
                    yield
                    for c in range(8):
                        kb.op("pe", lambda e: e.matmul(pg[:, 0:256], lhsT=W2A[:, c, 2432 + oc * 128:2560 + oc * 128], rhs=H[:, c, 128:384], start=(c == 0), stop=(c == 7)),
                              reads=[W2At, Ht[c]], writes=[pgt], sig=(c == 7))
                    yield
                    SG, SGt = SGp.next()
                    kb.op("act", lambda e: e.activation(out=SG[:, :], in_=pg[:, 0:256], func=AF.Sigmoid), reads=[pgt], writes=[SGt])
                    yield
                    MAo, MAot = MAp.next()
                    kb.op("dve", lambda e: e.tensor_tensor(out=MAo[:, :], in0=py[:, 0:256], in1=SG[:, :], op=ALU.mult), reads=[pyt, SGt], writes=[MAot])
                    yield
                    kb.dma("sp", MAd[oc * 128:(oc + 1) * 128, ch * 256:(ch + 1) * 256], MAo[:, :], reads=[MAot])
                for w in range(4):
                    wave([ch_out(2 * w), ch_out(2 * w + 1)])
                    if w == 1:
                        yield
                yield
            run_pipe([(lambda ch=ch: body_a(ch)) for ch in range(LIM["a_chunks"])], 7)
            kb.barrier()

        for ph in _phase("2B"):
            def s2(name, shape, dt):
                return sb("b_" + name, shape, dt, ph)
            W2B = s2("W2B", [128, 8, 256], BF16); W2Bt = Trk()
            WUQ = s2("WUQ", [128, 2, 1536], BF16); WUQt = Trk()
            WUQS = s2("WUQS", [128, 2, 1536], BF16); WUQSt = Trk()
            kb.dma("pool", W2B[:, :, :], wview(w2b), writes=[W2Bt])
            kb.dma("pool", WUQ[:, :, :], wview(wuq), writes=[WUQt])
            kb.dma("pool", WUQS[:, :, :], wview(wuqs), writes=[WUQSt])
            X = [s2("X%d" % i, [128, 8, 512], F32) for i in range(2)]; Xt = [Trk(), Trk()]
            SQ = s2("SQ", [128, 8, 512], BF16); SQt = Trk()
            RSs = [(s2("RS%d" % i, [128, 512], F32), Trk()) for i in range(2)]
            Hs = [(s2("H%d" % i, [128, 8, 512], BF16), [Trk() for _ in range(8)]) for i in range(2)]
            CQNs = [(s2("CQN%d" % i, [128, 2, 512], BF16), Trk()) for i in range(2)]
            POSIs = POSIg
            TBs = [dict(ANG=s2("ANG%d" % i, [96, 512], F32), KK=s2("KK%d" % i, [96, 512], F32), SN=s2("SN%d" % i, [96, 512], F32), CO=s2("CO%d" % i, [96, 512], F32),
                        tt={k: Trk() for k in ("ang", "kk", "sn", "co")}) for i in range(2)]
            sqp = RR([(s2("sq%d" % i, [128, 512], BF16), Trk()) for i in range(4)])
            rsp = RR([(s2("rs%d" % i, [128, 512], F32), Trk()) for i in range(4)])
            QNr = RR([(s2("QNr%d" % i, [96, 512], F32), Trk()) for i in range(3)])
            QSr = RR([(s2("QSr%d" % i, [96, 512], F32), Trk()) for i in range(3)])
            Qo = RR([(s2("Qo%d" % i, [96, 512], BF16), Trk()) for i in range(4)])
            PSg = RR([(pst("b_ps%d" % i, [128, 512], ph), Trk()) for i in range(8)])
            nb = PSg.next
            xs4 = xsT.rearrange("(c p) (k t) -> p c k t", p=128, t=384)

            def loadx(j, Xd, Xdt):
                for u in range(2):
                    kb.dma("act", Xd[:, :, u * 256:(u + 1) * 256], xs4[:, :, 2 * j + u, 128:384], writes=[Xdt])
            def body_b(j):
                Xc, Xct = X[j % 2], Xt[j % 2]
                RS, RSt = RSs[j % 2]; H, Ht = Hs[j % 2]; CQN, CQNt = CQNs[j % 2]; POSI, POSIt = POSIs[j % 2]
                TB = TBs[j % 2]; ANG, KK, SN, CO, tt = TB["ANG"], TB["KK"], TB["SN"], TB["CO"], TB["tt"]
                if j == 0:
                    loadx(0, X[0], Xt[0])
                    kb.dma("act", POSIs[0][0][64:96, :], poso[:, 0:512], writes=[POSIs[0][1]])
                if j + 1 < 8:
                    loadx(j + 1, X[(j + 1) % 2], Xt[(j + 1) % 2])
                    kb.dma("act", POSIs[(j + 1) % 2][0][64:96, :], poso[:, (j + 1) * 512:(j + 2) * 512], writes=[POSIs[(j + 1) % 2][1]])
                norm1024(Xc, Xct, H, Ht, 512, SQ, SQt, RS, RSt, C_GN, nb)
                yield
                rope_tables(POSI, POSIt, 512, ANG, KK, SN, CO, tt)
                COg, SNg = TB["KK"], TB["ANG"]
                kb.op("dve", lambda e: e.tensor_scalar(out=COg[64:96, :], in0=CO[64:96, :], scalar1=gn(C_GBQ, slice(64, 96)), scalar2=None, op0=ALU.mult),
                      reads=[tt["co"], tt["kk"], CTt], writes=[tt["kk"]])
                kb.op("dve", lambda e: e.tensor_scalar(out=SNg[64:96, :], in0=SN[64:96, :], scalar1=gn(C_GBQS, slice(64, 96)), scalar2=None, op0=ALU.mult),
                      reads=[tt["sn"], tt["ang"], CTt], writes=[tt["ang"]])
                yield
                pz = []
                for cc in range(2):
                    ps, pt = nb()
                    for c in range(8):
                        kb.op("pe", lambda e: e.matmul(ps[:, :], lhsT=W2B[:, c, cc * 128:(cc + 1) * 128], rhs=H[:, c, :], start=(c == 0), stop=(c == 7)),
                              reads=[W2Bt, Ht[c]], writes=[pt], sig=(c == 7))
                    pz.append((ps, pt))
                sqs = []
                for cc in range(2):
                    sq, sqt = sqp.next()
                    kb.op("act", lambda e: e.activation(out=sq[:, :], in_=pz[cc][0][:, :], func=AF.Square), reads=[pz[cc][1]], writes=[sqt])
                    sqs.append((sq, sqt))
                ps2, p2t = nb()
                for cc in range(2):
                    kb.op("pe", lambda e: e.matmul(ps2[:, :], lhsT=ONES, rhs=sqs[cc][0][:, :], start=(cc == 0), stop=(cc == 1)),
                          reads=[sqs[cc][1], CMATt], writes=[p2t], sig=(cc == 1))
                rs, rst = rsp.next()
                kb.op("act", lambda e: e.activation(out=rs[:, :], in_=ps2[:, :], func=AF.Ln, scale=1.0 / 256, bias=gn(C_EPS)), reads=[p2t, CTt], writes=[rst])
                kb.op("act", lambda e: e.activation(out=rs[:, :], in_=rs[:, :], func=AF.Exp, scale=-0.5), reads=[rst], writes=[rst])
                for cc in range(2):
                    kb.op("dve", lambda e: e.scalar_tensor_tensor(out=CQN[:, cc, :], in0=pz[cc][0][:, :], scalar=gn(C_GCQ + cc), in1=rs[:, :], op0=ALU.mult, op1=ALU.mult),
                          reads=[pz[cc][1], rst, CTt], writes=[CQNt])
                yield
                R = slice(64, 96)
                def ch_q(h):
                    pq, pqt = nb(); pw, pwt = nb()
                    for cc in range(2):
                        kb.op("pe", lambda e: e.matmul(pq[0:96, :], lhsT=WUQ[:, cc, h * 96:(h + 1) * 96], rhs=CQN[:, cc, :], start=(cc == 0), stop=(cc == 1)),
                              reads=[WUQt, CQNt], writes=[pqt], sig=(cc == 1))
                    for cc in range(2):
                        kb.op("pe", lambda e: e.matmul(pw[0:96, :], lhsT=WUQS[:, cc, h * 96:(h + 1) * 96], rhs=CQN[:, cc, :], start=(cc == 0), stop=(cc == 1)),
                              reads=[WUQSt, CQNt], writes=[pwt], sig=(cc == 1))
                    yield
                    o = []
                    yield from rstd_gen(pq, pqt, slice(0, 96), 512, ONE96, 1.0 / 96, sqp, rsp, nb, o)
                    rs, rst = o[0]
                    qo, qot = Qo.next(); qn, qnt = QNr.next(); qw, qwt = QSr.next()
                    kb.op("dve", lambda e: e.scalar_tensor_tensor(out=qo[0:64, :], in0=pq[0:64, :], scalar=gn(C_GBQ, slice(0, 64)), in1=rs[0:64, :], op0=ALU.mult, op1=ALU.mult),
                          reads=[pqt, rst, CTt], writes=[qot])
                    yield
                    kb.op("dve", lambda e: e.tensor_tensor(out=qn[R, :], in0=pq[R, :], in1=COg[R, :], op=ALU.mult), reads=[pqt, tt["kk"]], writes=[qnt])
                    yield
                    kb.op("dve", lambda e: e.tensor_tensor(out=qw[R, :], in0=pw[R, :], in1=SNg[R, :], op=ALU.mult), reads=[pwt, tt["ang"]], writes=[qwt])
                    yield
                    kb.op("pool", lambda e: e.tensor_tensor(out=qn[R, :], in0=qn[R, :], in1=qw[R, :], op=ALU.add), reads=[qnt, qwt], writes=[qnt])
                    yield
                    kb.op("dve", lambda e: e.tensor_tensor(out=qo[R, :], in0=qn[R, :], in1=rs[R, :], op=ALU.mult), reads=[qnt, rst], writes=[qot])
                    yield
                    kb.dma("sp", Qd[h, :, j * 512:(j + 1) * 512], qo[:, :], reads=[qot])
                for w in range(8):
                    wave([ch_q(2 * w), ch_q(2 * w + 1)])
                    yield
            run_pipe([(lambda j=j: body_b(j)) for j in range(8)], 4)
            kb.barrier()

        kvs = ExitStack()
        Kbuf = [sb("Kbuf%d" % i, [96, 8192], BF16, kvs) for i in range(2)]
        Kbt_n = [Trk(), Trk()]
        Kbt_r = [Trk(), Trk()]
        for ph in _phase("1"):
            def s2(name, shape, dt):
                return sb("c_" + name, shape, dt, ph)
            W1 = s2("W1", [128, 8, 320], BF16); W1t = Trk()
            WUK = s2("WUK", [128, 1024], BF16); WUKt = Trk()
            WUV = s2("WUV", [128, 1024], BF16); WUVt = Trk()
            kb.dma("pool", W1[:, :, :], wview(w1), writes=[W1t])
            kb.dma("pool", WUK[:, :], wuk[:, :], writes=[WUKt])
            kb.dma("pool", WUV[:, :], wuv[:, :], writes=[WUVt])
            X = [s2("X%d" % i, [128, 8, 512], F32) for i in range(2)]; Xt = [Trk(), Trk()]
            SQ = s2("SQ", [128, 8, 512], BF16); SQt = Trk()
            RSs = [(s2("RS%d" % i, [128, 512], F32), Trk()) for i in range(2)]
            Hs = [(s2("H%d" % i, [128, 8, 512], BF16), [Trk() for _ in range(8)]) for i in range(2)]
            CKVp = RR([(s2("CKV%d" % i, [128, 512], BF16), Trk()) for i in range(2)])
            POSIs = POSIg
            TBs = [dict(ANG=s2("ANG%d" % i, [96, 512], F32), KK=s2("KK%d" % i, [96, 512], F32), SN=s2("SN%d" % i, [96, 512], F32), CO=s2("CO%d" % i, [96, 512], F32),
                        tt={k: Trk() for k in ("ang", "kk", "sn", "co")}) for i in range(2)]
            KRs = [(s2("KRN%d" % i, [96, 512], F32), s2("KRS%d" % i, [96, 512], F32), Trk(), Trk()) for i in range(2)]
            sqp = RR([(s2("sq%d" % i, [128, 512], BF16), Trk()) for i in range(4)])
            rsp = RR([(s2("rs%d" % i, [128, 512], F32), Trk()) for i in range(4)])
            KNp = RR([(s2("KN%d" % i, [128, 512], BF16), Trk()) for i in range(4)])
            VST = [s2("VST%d" % i, [128, 16, 4, 128], BF16) for i in range(2)]; VSTt = [Trk(), Trk()]
            PSg = RR([(pst("c_ps%d" % i, [128, 512], ph), Trk()) for i in range(8)])
            nb = PSg.next
            for i in range(2):
                kb.op("pool", lambda e: e.memset(VST[i][:, :, :, :], 1.0), writes=[VSTt[i]])
            xb3 = wview(xbT)
            Vd4 = Vd.rearrange("h b t e -> t h b e")
            R = slice(64, 96)

            def body_c(bt):
                ts = slice(bt * 512, (bt + 1) * 512)
                Xc, Xct = X[bt % 2], Xt[bt % 2]
                RS, RSt = RSs[bt % 2]; H, Ht = Hs[bt % 2]; POSI, POSIt = POSIs[bt % 2]
                TB = TBs[bt % 2]; ANG, KK, SN, CO, tt = TB["ANG"], TB["KK"], TB["SN"], TB["CO"], TB["tt"]
                KRN, KRS, krnt, krst = KRs[bt % 2]
                if bt == 0:
                    kb.dma("act", X[0][:, :, :], xb3[:, :, 0:512], writes=[Xt[0]])
                    kb.dma("act", POSIs[0][0][64:96, :], posb[:, 0:512], writes=[POSIs[0][1]])
                if bt + 1 < 16:
                    kb.dma("act", X[(bt + 1) % 2][:, :, :], xb3[:, :, (bt + 1) * 512:(bt + 2) * 512], writes=[Xt[(bt + 1) % 2]])
                    kb.dma("act", POSIs[(bt + 1) % 2][0][64:96, :], posb[:, (bt + 1) * 512:(bt + 2) * 512], writes=[POSIs[(bt + 1) % 2][1]])
                norm1024(Xc, Xct, H, Ht, 512, SQ, SQt, RS, RSt, C_GN, nb)
                yield
                rope_tables(POSI, POSIt, 512, ANG, KK, SN, CO, tt)
                yield
                pc, pct = nb(); pr_, prt = nb(); pw, pwt = nb()
                for (pp_, ppt, c0, m) in ((pc, pct, 0, 128), (pr_, prt, 128, 96), (pw, pwt, 224, 96)):
                    for c in range(8):
                        kb.op("pe", lambda e: e.matmul(pp_[0:m, :], lhsT=W1[:, c, c0:c0 + m], rhs=H[:, c, :], start=(c == 0), stop=(c == 7)),
                              reads=[W1t, Ht[c]], writes=[ppt], sig=(c == 7))
                CKV, CKVt = CKVp.next()

                def ch_ckv():
                    o = []
                    yield from rstd_gen(pc, pct, slice(0, 128), 512, ONES, 1.0 / 128, sqp, rsp, nb, o)
                    rs, rst = o[0]
                    kb.op("dve", lambda e: e.scalar_tensor_tensor(out=CKV[:, :], in0=pc[:, :], scalar=gn(C_GCKV), in1=rs[:, :], op0=ALU.mult, op1=ALU.mult),
                          reads=[pct, rst, CTt], writes=[CKVt])

                def ch_kr():
                    o = []
                    yield from rstd_gen(pr_, prt, slice(0, 96), 512, KR32, 1.0 / 32, sqp, rsp, nb, o)
                    rs, rst = o[0]
                    kb.op("dve", lambda e: e.scalar_tensor_tensor(out=KRN[R, :], in0=pr_[R, :], scalar=gn(C_GKR, R), in1=rs[R, :], op0=ALU.mult, op1=ALU.mult),
                          reads=[prt, rst, CTt], writes=[krnt])
                    kb.op("dve", lambda e: e.scalar_tensor_tensor(out=KRS[R, :], in0=pw[R, :], scalar=gn(C_GKRS, R), in1=rs[R, :], op0=ALU.mult, op1=ALU.mult),
                          reads=[pwt, rst, CTt], writes=[krst])
                    yield
                    kb.op("pool", lambda e: e.tensor_tensor(out=KRN[R, :], in0=KRN[R, :], in1=CO[R, :], op=ALU.mult), reads=[krnt, tt["co"]], writes=[krnt])
                    kb.op("pool", lambda e: e.tensor_tensor(out=KRS[R, :], in0=KRS[R, :], in1=SN[R, :], op=ALU.mult), reads=[krst, tt["sn"]], writes=[krst])
                    yield
                    for i in range(2):
                        kb.op("pool", lambda e: e.tensor_tensor(out=Kbuf[i][R, ts], in0=KRN[R, :], in1=KRS[R, :], op=ALU.add), reads=[krnt, krst], writes=[Kbt_r[i]])
                wave([ch_ckv(), ch_kr()])
                yield

                def ch_kn(i):
                    pk, pkt = nb()
                    kb.op("pe", lambda e: e.matmul(pk[:, :], lhsT=WUK[:, i * 128:(i + 1) * 128], rhs=CKV[:, :], start=True, stop=True), reads=[WUKt, CKVt], writes=[pkt])
                    yield
                    o = []
                    yield from rstd_gen(pk, pkt, slice(0, 128), 512, BLK64, 1.0 / 64, sqp, rsp, nb, o)
                    rs, rst = o[0]
                    KN, KNt = KNp.next()
                    kb.op("dve", lambda e: e.scalar_tensor_tensor(out=KN[:, :], in0=pk[:, :], scalar=gn(C_GBK), in1=rs[:, :], op0=ALU.mult, op1=ALU.mult),
                          reads=[pkt, rst, CTt], writes=[KNt])
                    yield
                    kb.dma("sp", Kd[i * 128:(i + 1) * 128, ts], KN[:, :], reads=[KNt])
                for w in range(2):
                    wave([ch_kn(i) for i in range(4 * w, 4 * w + 4)])
                    yield
                VS, VSt = VST[bt % 2], VSTt[bt % 2]

                def ch_v(blk, half):
                    pv_, pvt = nb()
                    kb.op("pe", lambda e: e.matmul(pv_[:, :], lhsT=CKV[:, blk * 128:(blk + 1) * 128], rhs=WUV[:, half * 512:(half + 1) * 512], start=True, stop=True),
                          reads=[WUVt, CKVt], writes=[pvt])
                    yield
                    p4 = pv_[:, :].rearrange("p (a b e) -> p a b e", a=4, b=2)
                    for par in range(2):
                        kb.op("dve", lambda e: e.tensor_copy(out=VS[:, half * 8 + par:half * 8 + 8:2, blk, par * 64:par * 64 + 64], in_=p4[:, :, par, :]),
                              reads=[pvt], writes=[VSt])
                        yield
                for w in range(2):
                    wave([ch_v(blk, half) for blk in (2 * w, 2 * w + 1) for half in range(2)])
                    yield
                kb.dma("sp", Vd4[:, :, bt, :], VS[:, :, :, :].rearrange("p h b e -> p h (b e)"), reads=[VSt])
            run_pipe([(lambda bt=bt: body_c(bt)) for bt in range(16)], 3)
            kb.barrier()

        for ph in _phase("3"):
            def s2(name, shape, dt):
                return sb("d_" + name, shape, dt, ph)
            Vbuf = [s2("Vbuf%d" % i, [128, 64, 128], BF16) for i in range(2)]; Vbt = [Trk(), Trk()]
            CM = s2("CM", [128, 8, 512], BF16); CMt = Trk()
            kb.dma("pool", CM[:, :, :], cmk.rearrange("p (r q) -> p r q", r=8), writes=[CMt])
            Qt = RR([(s2("Qt%d" % i, [96, 512], BF16), Trk()) for i in range(3)])
            Pp = RR([(s2("P%d" % i, [128, 2, 512], BF16), Trk()) for i in range(3)])
            DENp = RR([(s2("DEN%d" % i, [128, 512], F32), Trk()) for i in range(2)])
            OBp = RR([(s2("OB%d" % i, [128, 512], F32), Trk()) for i in range(2)])
            SPp = RR([(pst("d_S%d" % i, [128, 2, 512], ph), Trk()) for i in range(3)])
            OPp = RR([(pst("d_O%d" % i, [128, 512], ph), Trk()) for i in range(2)])
            Kd3 = Kd.rearrange("(h r) t -> h r t", r=64)

            def loadkv(h):
                b = h % 2
                kb.dma("sp", Kbuf[b][0:64, :], Kd3[h], writes=[Kbt_n[b]])
                kb.dma("sp", Vbuf[b][:, :, :].rearrange("p (b k) e -> p b (k e)", k=4), Vd[h].rearrange("b t e -> t b e"), writes=[Vbt[b]])
            SC = 96.0 ** -0.5
            iters = [(h, j) for h in range(16) for j in range(8)]
            groups = [(i, gi) for i, (h, j) in enumerate(iters) for gi in range(4 * j + 4)]
            T = len(groups)
            Qs = {}; Os = {}; Ss = {}
            Qs[0] = Qt.next()
            kb.dma("sp", Qs[0][0][:, :], Qd[0, :, 0:512], writes=[Qs[0][1]])
            loadkv(0)

            def qk(t):
                i, gi = groups[t]
                h, j = iters[i]
                b = h % 2
                if gi == 0 and i + 1 < len(iters):
                    Qs[i + 1] = Qt.next()
                    h2, j2 = iters[i + 1]
                    kb.dma("sp", Qs[i + 1][0][:, :], Qd[h2, :, j2 * 512:(j2 + 1) * 512], writes=[Qs[i + 1][1]])
                Q, Qtt = Qs[i]
                S, St = SPp.next()
                ngq = 4 * j + 4
                qlo = (0, 0, 256, 256)[max(0, gi - (ngq - 4))]
                for u in range(2):
                    kbi = gi * 2 + u
                    kb.op("pe", lambda e: e.matmul(S[:, u, qlo:512], lhsT=Kbuf[b][0:96, kbi * 128:(kbi + 1) * 128], rhs=Q[0:96, qlo:512], start=True, stop=True),
                          reads=[Kbt_n[b], Kbt_r[b], Qtt], writes=[St], sig=(u == 1))
                Ss[t] = (S, St)

            def rest(t):
                i, gi = groups[t]
                h, j = iters[i]
                b = h % 2; par = h % 2
                ng = 4 * j + 4
                osl = slice(par * 64, par * 64 + 64); ssl = slice((1 - par) * 64, (1 - par) * 64 + 64)
                if gi == 0:
                    Os[i] = OPp.next()
                    if j == 0 and h + 1 < 16:
                        loadkv(h + 1)
                O, Ot = Os[i]
                S, St = Ss.pop(t)
                P, Pt = Pp.next()
                qlo = (0, 0, 256, 256)[max(0, gi - (ng - 4))]
                kb.op("act", lambda e: e.activation(out=P[:, :, qlo:512], in_=S[:, :, qlo:512], func=AF.Exp, scale=SC), reads=[St], writes=[Pt])
                if gi >= ng - 4:
                    r0 = (gi - (ng - 4)) * 2
                    kb.op("dve", lambda e: e.tensor_tensor(out=P[:, :, qlo:512], in0=P[:, :, qlo:512], in1=CM[:, r0:r0 + 2, qlo:512], op=ALU.mult), reads=[Pt, CMt], writes=[Pt])
                for u in range(2):
                    kbi = gi * 2 + u
                    kb.op("pe", lambda e: e.matmul(O[:, qlo:512], lhsT=Vbuf[b][:, kbi, :], rhs=P[:, u, qlo:512], start=(kbi == 0), stop=(kbi == 2 * ng - 1)),
                          reads=[Vbt[b], Pt], writes=[Ot], sig=(u == 1))
                if gi == ng - 1:
                    DEN, DENt = DENp.next(); OB, OBt = OBp.next()
                    kb.op("dve", lambda e: e.reciprocal(out=DEN[osl, :], in_=O[ssl, :]), reads=[Ot], writes=[DENt])
                    kb.op("dve", lambda e: e.tensor_tensor(out=OB[osl, :], in0=O[osl, :], in1=DEN[osl, :], op=ALU.mult), reads=[Ot, DENt], writes=[OBt])
                    kb.dma("sp", OBd[h * 64:(h + 1) * 64, j * 512:(j + 1) * 512], OB[osl, :], reads=[OBt])
                    del Os[i]
            qk(0); qk(1)
            for t in range(T):
                if t + 2 < T:
                    qk(t + 2)
                rest(t)
            kb.barrier()

        kvs.close()
        for ph in _phase("4"):
            def s2(name, shape, dt):
                return sb("e_" + name, shape, dt, ph)
            NT = 256
            W4 = s2("W4", [128, 8, 2048], BF16); W4t = Trk()
            WOB = s2("WOB", [128, 8, 1024], BF16); WOBt = Trk()
            WOUT = s2("WOUT", [128, 8, 1024], BF16); WOUTt = Trk()
            WPG = s2("WPG", [128, 8, 1024], BF16); WPGt = Trk()
            WPP = s2("WPP", [128, 2, 1024], BF16); WPPt = Trk()
            kb.dma("pool", W4[:, :, :], wview(w4), writes=[W4t])
            kb.dma("pool", WOB[:, :, :], wview(wob), writes=[WOBt])
            kb.dma("pool", WOUT[:, :, :], wview(wout), writes=[WOUTt])
            kb.dma("pool", WPG[:, :, :], wview(wpg), writes=[WPGt])
            kb.dma("pool", WPP[:, :, :], wview(wpp), writes=[WPPt])
            X = [s2("X%d" % i, [128, 8, NT], F32) for i in range(3)]; Xt = [[Trk() for _ in range(8)] for _ in range(3)]; Xall = [Trk(), Trk(), Trk()]
            SQ = s2("SQ", [128, 8, NT], BF16); SQt = Trk()
            RSs = [(s2("RS%d" % i, [128, NT], F32), Trk()) for i in range(2)]
            Hs = [(s2("H%d" % i, [128, 8, NT], BF16), [Trk() for _ in range(8)]) for i in range(2)]
            YBs = [(s2("YB%d" % i, [128, 8, NT], BF16), [Trk() for _ in range(8)]) for i in range(2)]
            MGs = [(s2("MG%d" % i, [128, 8, NT], BF16), [Trk() for _ in range(8)]) for i in range(2)]
            EMs = [(s2("EM%d" % i, [128, 8, NT], F32), [Trk() for _ in range(8)]) for i in range(2)]
            GTs = [(s2("GT%d" % i, [128, 8, NT], F32), [Trk() for _ in range(8)]) for i in range(2)]
            PBs = [(s2("PB%d" % i, [128, 2, NT], BF16), Trk()) for i in range(2)]
            PF = [s2("PF%d" % i, [128, 2, NT], F32) for i in range(3)]; PFt = [Trk(), Trk(), Trk()]
            OBl = RR([(s2("OBl%d" % i, [128, NT], F32), Trk()) for i in range(4)])
            MAl = RR([(s2("MAl%d" % i, [128, NT], F32), Trk()) for i in range(4)])
            SZp = RR([(s2("SZ%d" % i, [128, NT], F32), Trk()) for i in range(4)])
            SGp = RR([(s2("SG%d" % i, [128, NT], F32), Trk()) for i in range(4)])
            T1p = RR([(s2("T1%d" % i, [128, NT], F32), Trk()) for i in range(4)])
            PSg = RR([(pst("e_ps%d" % i, [128, 512], ph), Trk()) for i in range(8)])
            nb = PSg.next
            xs4 = xsT.rearrange("(c p) (k t) -> p c k t", p=128, t=384)
            p3 = pT.rearrange("(c p) t -> p c t", p=128)

            def loadx(ti, Xd, Xdt, Xa):
                kb.dma("act", Xd[:, :, :], xs4[:, :, ti, 128:384], writes=Xdt + [Xa])
            def body_e(ti):
                ts = slice(ti * NT, (ti + 1) * NT)
                Xc, Xct, Xa = X[ti % 3], Xt[ti % 3], Xall[ti % 3]
                PFc, PFct = PF[ti % 3], PFt[ti % 3]
                RS, RSt = RSs[ti % 2]; H, Ht = Hs[ti % 2]; YB, YBt = YBs[ti % 2]; MG, MGt = MGs[ti % 2]
                EM, EMt = EMs[ti % 2]; GT, GTt = GTs[ti % 2]; PB, PBt = PBs[ti % 2]
                if ti == 0:
                    loadx(0, X[0], Xt[0], Xall[0])
                    kb.dma("act", PF[0][:, :, :], p3[:, :, 0:NT], writes=[PFt[0]])
                if ti + 1 < 16:
                    n3 = (ti + 1) % 3
                    loadx(ti + 1, X[n3], Xt[n3], Xall[n3])
                    kb.dma("act", PF[n3][:, :, :], p3[:, :, (ti + 1) * NT:(ti + 2) * NT], writes=[PFt[n3]])
                norm1024(Xc, Xa, H, Ht, NT, SQ, SQt, RS, RSt, C_GN, nb)
                yield
                def ch_zb(c):
                    ob, obt = OBl.next()
                    kb.dma("pool", ob[:, :], OBd[c * 128:(c + 1) * 128, ts], writes=[obt])
                    ps, pt = nb()
                    for k in range(8):
                        kb.op("pe", lambda e: e.matmul(ps[:, 0:NT], lhsT=W4[:, k, c * 128:(c + 1) * 128], rhs=H[:, k, :], start=(k == 0), stop=(k == 7)),
                              reads=[W4t, Ht[k]], writes=[pt], sig=(k == 7))
                    yield
                    sz, szt = SZp.next()
                    kb.op("act", lambda e: e.activation(out=sz[:, :], in_=ps[:, 0:NT], func=AF.Silu), reads=[pt], writes=[szt])
                    yield
                    kb.op("dve", lambda e: e.tensor_tensor(out=YB[:, c, :], in0=sz[:, :], in1=ob[:, :], op=ALU.mult), reads=[szt, obt], writes=[YBt[c]])
                for w in range(2):
                    wave([ch_zb(c) for c in range(4 * w, 4 * w + 4)])
                yield

                def ch_mg(oc):
                    ma, mat = MAl.next()
                    kb.dma("pool", ma[:, :], MAd[oc * 128:(oc + 1) * 128, ts], writes=[mat])
                    py, pyt = nb(); pg, pgt = nb()
                    for k in range(8):
                        kb.op("pe", lambda e: e.matmul(py[:, 0:NT], lhsT=WOB[:, k, oc * 128:(oc + 1) * 128], rhs=YB[:, k, :], start=(k == 0), stop=(k == 7)),
                              reads=[WOBt, YBt[k]], writes=[pyt], sig=(k == 7))
                    yield
                    for k in range(8):
                        kb.op("pe", lambda e: e.matmul(pg[:, 0:NT], lhsT=W4[:, k, 1024 + oc * 128:1152 + oc * 128], rhs=H[:, k, :], start=(k == 0), stop=(k == 7)),
                              reads=[W4t, Ht[k]], writes=[pgt], sig=(k == 7))
                    yield
                    sg, sgt = SGp.next()
                    kb.op("act", lambda e: e.activation(out=sg[:, :], in_=pg[:, 0:NT], func=AF.Sigmoid), reads=[pgt], writes=[sgt])
                    yield
                    t1, t1t = T1p.next()
                    kb.op("dve", lambda e: e.tensor_tensor(out=t1[:, :], in0=py[:, 0:NT], in1=sg[:, :], op=ALU.mult), reads=[pyt, sgt], writes=[t1t])
                    yield
                    kb.op("dve", lambda e: e.tensor_tensor(out=MG[:, oc, :], in0=t1[:, :], in1=ma[:, :], op=ALU.add), reads=[t1t, mat], writes=[MGt[oc]])
                for w in range(2):
                    wave([ch_mg(oc) for oc in range(4 * w, 4 * w + 4)])
                yield

                def ch_x(oc):
                    px, pxt = nb()
                    for k in range(8):
                        kb.op("pe", lambda e: e.matmul(px[:, 0:NT], lhsT=WOUT[:, k, oc * 128:(oc + 1) * 128], rhs=MG[:, k, :], start=(k == 0), stop=(k == 7)),
                              reads=[WOUTt, MGt[k]], writes=[pxt], sig=(k == 7))
                    yield
                    kb.op("dve", lambda e: e.tensor_tensor(out=Xc[:, oc, :], in0=px[:, 0:NT], in1=Xc[:, oc, :], op=ALU.add), reads=[pxt, Xct[oc], Xa], writes=[Xct[oc]])
                for w in range(2):
                    wave([ch_x(oc) for oc in range(4 * w, 4 * w + 4)])
                    yield
                kb.op("act", lambda e: e.activation(out=PB[:, :, :], in_=PFc[:, :, :], func=AF.Copy), reads=[PFct], writes=[PBt])

                def ch_e(oc):
                    pe_, pet = nb()
                    for k in range(2):
                        kb.op("pe", lambda e: e.matmul(pe_[:, 0:NT], lhsT=WPP[:, k, oc * 128:(oc + 1) * 128], rhs=PB[:, k, :], start=(k == 0), stop=(k == 1)),
                              reads=[WPPt, PBt], writes=[pet], sig=(k == 1))
                    yield
                    kb.op("dve", lambda e: e.tensor_copy(out=EM[:, oc, :], in_=pe_[:, 0:NT]), reads=[pet], writes=[EMt[oc]])
                for w in range(2):
                    wave([ch_e(oc) for oc in range(4 * w, 4 * w + 4)])
                    yield
                kb.op("act", lambda e: e.activation(out=SQ[:, :, :], in_=Xc[:, :, :], func=AF.Square), reads=Xct + [Xa], writes=[SQt])
                ps, pt = nb()
                for c in range(8):
                    kb.op("pe", lambda e: e.matmul(ps[:, 0:NT], lhsT=ONES, rhs=SQ[:, c, :], start=(c == 0), stop=(c == 7)), reads=[SQt, CMATt], writes=[pt], sig=(c == 7))
                kb.op("act", lambda e: e.activation(out=RS[:, :], in_=ps[:, 0:NT], func=AF.Ln, scale=1.0 / 1024, bias=gn(C_EPS)), reads=[pt, CTt], writes=[RSt])
                kb.op("act", lambda e: e.activation(out=RS[:, :], in_=RS[:, :], func=AF.Exp, scale=-0.5), reads=[RSt], writes=[RSt])
                for c in range(8):
                    kb.op("dve", lambda e: e.scalar_tensor_tensor(out=H[:, c, :], in0=Xc[:, c, :], scalar=gn(C_GPLE + c), in1=RS[:, :], op0=ALU.mult, op1=ALU.mult),
                          reads=[Xct[c], Xa, RSt, CTt], writes=[Ht[c]])
                yield
                def ch_g(oc):
                    pg, pgt = nb()
                    for k in range(8):
                        kb.op("pe", lambda e: e.matmul(pg[:, 0:NT], lhsT=WPG[:, k, oc * 128:(oc + 1) * 128], rhs=H[:, k, :], start=(k == 0), stop=(k == 7)),
                              reads=[WPGt, Ht[k]], writes=[pgt], sig=(k == 7))
                    yield
                    kb.op("act", lambda e: e.activation(out=GT[:, oc, :], in_=pg[:, 0:NT], func=AF.Sigmoid), reads=[pgt], writes=[GTt[oc]])
                for w in range(2):
                    wave([ch_g(oc) for oc in range(4 * w, 4 * w + 4)])
                yield
                kb.op("act", lambda e: e.activation(out=SQ[:, :, :], in_=EM[:, :, :], func=AF.Square), reads=EMt, writes=[SQt])
                ps, pt = nb()
                for c in range(8):
                    kb.op("pe", lambda e: e.matmul(ps[:, 0:NT], lhsT=ONES, rhs=SQ[:, c, :], start=(c == 0), stop=(c == 7)), reads=[SQt, CMATt], writes=[pt], sig=(c == 7))
                kb.op("act", lambda e: e.activation(out=RS[:, :], in_=ps[:, 0:NT], func=AF.Ln, scale=1.0 / 1024, bias=gn(C_EPS)), reads=[pt, CTt], writes=[RSt])
                kb.op("act", lambda e: e.activation(out=RS[:, :], in_=RS[:, :], func=AF.Exp, scale=-0.5), reads=[RSt], writes=[RSt])
                for oc in range(8):
                    kb.op("dve", lambda e: e.scalar_tensor_tensor(out=EM[:, oc, :], in0=EM[:, oc, :], scalar=gn(C_GPOST + oc), in1=RS[:, :], op0=ALU.mult, op1=ALU.mult),
                          reads=[EMt[oc], RSt, CTt], writes=[EMt[oc]])
                    kb.op("dve", lambda e: e.tensor_tensor(out=EM[:, oc, :], in0=EM[:, oc, :], in1=GT[:, oc, :], op=ALU.mult), reads=[EMt[oc], GTt[oc]], writes=[EMt[oc]])
                    kb.op("dve", lambda e: e.tensor_tensor(out=EM[:, oc, :], in0=EM[:, oc, :], in1=Xc[:, oc, :], op=ALU.add), reads=[EMt[oc], Xct[oc], Xa], writes=[EMt[oc]])
                yield
                kb.dma("sp", outT.rearrange("(c p) t -> p c t", p=128)[:, :, ts], EM[:, :, :], reads=EMt)
            run_pipe([(lambda ti=ti: body_e(ti)) for ti in range(16)], 5)
            kb.barrier()
        kb.finish()
    return nc


def _t5_bucket(d):
    d = np.asarray(d)
    dd = np.maximum(d, 1).astype(np.float32)
    large = 16 + (np.log(dd / 16) / math.log(128 / 16) * 16).astype(np.int32)
    large = np.minimum(large, 31)
    return np.where(d < 16, d, large)


def _own_chunks(par):
    res = []
    for j in range(8):
        res += [4 * j + (0 if par == 0 else 1), 4 * j + (3 if par == 0 else 2)]
    return res


_NC_CACHE = {}


def kernel(x, p, positions, norm_g, w_in, a_q_norm, a_k_norm, a_sinks, rel_bias, w_o_a,
           b_cq_norm, w_uq, b_ckv_norm, w_uk, w_uv, b_q_norm, b_k_norm, b_kr_norm, w_o_b,
           w_out, ple_norm_g, w_ple_gate, w_ple_proj, ple_post_g, _debug=False):
    f32 = np.float32
    x = np.asarray(x, f32); p = np.asarray(p, f32); positions = np.asarray(positions, np.int32)
    W = np.asarray(w_in, f32)[0]
    qa, ka, va, za = W[:, 0:1024], W[:, 1024:1152], W[:, 1152:1280], W[:, 1280:2304]
    cq, ckv, kr, zb = W[:, 2304:2560], W[:, 2560:2688], W[:, 2688:2720], W[:, 2720:3744]
    ga, gb = W[:, 3744:4768], W[:, 4768:5792]
    sw = np.concatenate([np.arange(16, 32), np.arange(0, 16)])
    kadup = np.concatenate([ka[:, 0:64], ka[:, 0:64], ka[:, 64:128], ka[:, 64:128]], axis=1)
    w2a = np.ascontiguousarray(np.concatenate([qa, kadup, va, za, ga], axis=1))
    z64 = np.zeros((1024, 64), f32)
    w1 = np.ascontiguousarray(np.concatenate([ckv, z64, kr, z64, kr[:, sw]], axis=1))
    w4 = np.ascontiguousarray(np.concatenate([zb, gb], axis=1))
    wuq_ = np.asarray(w_uq, f32)[0]
    wuq3 = wuq_.reshape(256, 16, 96)
    wuqs_ = np.ascontiguousarray(np.concatenate([wuq3[:, :, 0:64], wuq3[:, :, 64:96][:, :, sw]], axis=2).reshape(256, 1536))

    def col8(g):
        return np.asarray(g, f32).reshape(8, 128).T

    cst = np.zeros((128, 64), f32)
    cst[:, C_GN:C_GN + 8] = col8(norm_g[0]); cst[:, C_GPLE:C_GPLE + 8] = col8(ple_norm_g[0]); cst[:, C_GPOST:C_GPOST + 8] = col8(ple_post_g[0])
    cst[:, C_GAQ] = np.tile(np.asarray(a_q_norm, f32)[0], 2); cst[:, C_GAK] = np.tile(np.asarray(a_k_norm, f32)[0], 2)
    cst[:, C_GCQ:C_GCQ + 2] = np.asarray(b_cq_norm, f32)[0].reshape(2, 128).T
    cst[:, C_GCKV] = np.asarray(b_ckv_norm, f32)[0]
    bq = np.asarray(b_q_norm, f32)[0]
    cst[0:96, C_GBQ] = bq
    cst[0:96, C_GBQS] = np.concatenate([bq[0:64], bq[64:96][sw]])
    cst[:, C_GBK] = np.tile(np.asarray(b_k_norm, f32)[0], 2)
    bkr = np.asarray(b_kr_norm, f32)[0]
    cst[64:96, C_GKR] = bkr; cst[64:96, C_GKRS] = bkr[sw]
    cst[:, C_SINK:C_SINK + 16] = np.broadcast_to(np.asarray(a_sinks, f32)[0], (128, 16))
    inv_freq = (np.float32(10000.0) ** (-np.arange(0, 32, 2, dtype=np.float32) / np.float32(32))).astype(f32)
    cst[64:96, C_INVF] = np.tile(inv_freq, 2)
    cst[64:80, C_SSGN] = -1.0; cst[80:96, C_SSGN] = 1.0
    cst[:, C_EPS] = EPS; cst[:, C_HPI] = math.pi / 2
    cmat = np.zeros((128, 640), f32)
    cmat[np.arange(128), 512 + 127 - np.arange(128)] = 1.0
    cmat[:, 0:128] = 1.0
    cmat[0:64, 128:192] = 1.0; cmat[64:128, 192:256] = 1.0
    cmat[0:96, 256:352] = 1.0
    cmat[64:96, 384 + 64:384 + 96] = 1.0
    bsel = np.zeros((32, 128), f32)
    bsel[_t5_bucket(np.arange(128)), np.arange(128)] = 1.0
    hperm = [hg * 4 + 2 * i + par for hg in range(4) for par in range(2) for i in range(2)]
    relb = np.ascontiguousarray(np.asarray(rel_bias, f32)[:, hperm])

    in_maps = []
    metas = []
    for c in range(8):
        b, par = c // 2, c % 2
        chunks = _own_chunks(par)
        xb = x[b]
        xbT = np.ascontiguousarray(xb.T)
        segs = []
        hfl = np.ones((16, 128, 128), f32)
        own_idx = []
        for i, ch in enumerate(chunks):
            if ch == 0:
                segs.append(np.zeros((128, 1024), f32)); hfl[i] = 0.0
            else:
                segs.append(xb[ch * 256 - 128:ch * 256])
            segs.append(xb[ch * 256:(ch + 1) * 256])
            own_idx.append(np.arange(ch * 256, (ch + 1) * 256))
        own_idx = np.concatenate(own_idx)
        xsT = np.ascontiguousarray(np.concatenate(segs, axis=0).T)
        pT = np.ascontiguousarray(p[0, b][own_idx].T)
        posb = np.ascontiguousarray(np.broadcast_to(positions[b][None, :], (32, 8192))).astype(np.int32)
        poso = np.ascontiguousarray(np.broadcast_to(positions[b][own_idx][None, :], (32, 4096))).astype(np.int32)
        offs = [0, 1, 6, 7] if par == 0 else [2, 3, 4, 5]
        cm = np.zeros((128, 8, 512), f32)
        kk = np.arange(128)[:, None]; qi = np.arange(128)[None, :]
        for r in range(8):
            for qb in range(4):
                if r < offs[qb]:
                    cm[:, r, qb * 128:(qb + 1) * 128] = 1.0
                elif r == offs[qb]:
                    cm[:, r, qb * 128:(qb + 1) * 128] = (kk <= qi).astype(f32)
        in_maps.append(dict(
            xbT=xbT, xsT=xsT, pT=pT, posb=posb, poso=poso, hfl=hfl, cst=cst, cmat=cmat, bsel=bsel, relb=relb,
            cmk=np.ascontiguousarray(cm.reshape(128, 4096)),
            w2a=w2a, woa=np.ascontiguousarray(np.asarray(w_o_a, f32)[0]), w2b=np.ascontiguousarray(cq),
            wuq=np.ascontiguousarray(wuq_), wuqs=wuqs_, w1=w1,
            wuk=np.ascontiguousarray(np.asarray(w_uk, f32)[0]), wuv=np.ascontiguousarray(np.asarray(w_uv, f32)[0]),
            w4=w4, wob=np.ascontiguousarray(np.asarray(w_o_b, f32)[0]), wout=np.ascontiguousarray(np.asarray(w_out, f32)[0]),
            wpg=np.ascontiguousarray(np.asarray(w_ple_gate, f32)[0]), wpp=np.ascontiguousarray(np.asarray(w_ple_proj, f32)[0]),
        ))
        metas.append((b, own_idx))
    key = bool(_debug)
    if key not in _NC_CACHE:
        _NC_CACHE[key] = build(debug=key)
    nc = _NC_CACHE[key]
    res = run_bass_kernel_spmd(nc, in_maps, core_ids=list(range(8)))
    out = np.empty((4, 8192, 1024), f32)
    for c in range(8):
        b, own_idx = metas[c]
        out[b, own_idx, :] = np.asarray(res.results[c]["outT"]).T
    if _debug:
        return out, res.results, metas
    return out
```

```python
import math
from contextlib import ExitStack
import numpy as np
import ml_dtypes
import concourse.bass as bass
import concourse.mybir as mybir
from concourse.bass_utils import run_bass_kernel_spmd

F32 = mybir.dt.float32
BF16 = mybir.dt.bfloat16
I32 = mybir.dt.int32
AF = mybir.ActivationFunctionType
ALU = mybir.AluOpType

DEBUG = False
EPS = 1e-6
C_GN, C_GPLE, C_GPOST, C_GAQ, C_GAK, C_GCQ, C_GCKV, C_GBQ, C_GBQS, C_GBK, C_GKR, C_GKRS = 0, 8, 16, 24, 25, 26, 28, 29, 30, 31, 32, 33
C_SINK, C_INVF, C_SSGN, C_EPS, C_HPI = 34, 50, 51, 52, 53
TWO_PI = 2.0 * math.pi
CW1 = 6.28125
_r = np.float32(TWO_PI - 6.28125)
CW2 = float(np.frombuffer(np.uint32(np.frombuffer(_r.tobytes(), np.uint32)[0] & np.uint32(0xFFFFF000)).tobytes(), np.float32)[0])
CW3 = float(TWO_PI - 6.28125 - CW2)
MAGIC = 12582912.0
PI_IN = 3.1415925


PHASES = {"2A", "2B", "1", "3", "4"}
DEBUG_KEEP = set()
LIM = {"a_chunks": 16, "a_stage": 9, "a_setup": 1}


def _phase(name):
    if name in PHASES:
        with ExitStack() as ph:
            yield ph


def run_pipe(bodies, stagger):
    active = []
    nxt = 0
    step = 0
    start_step = 0
    while nxt < len(bodies) or active:
        if nxt < len(bodies) and len(active) < 2 and (not active or step >= start_step + stagger):
            active.append(bodies[nxt]())
            nxt += 1
            start_step = step
        for g_ in list(active):
            try:
                next(g_)
            except StopIteration:
                active.remove(g_)
        step += 1


def wave(gens):
    gens = list(gens)
    while gens:
        for g_ in list(gens):
            try:
                next(g_)
            except StopIteration:
                gens.remove(g_)


class Trk:
    __slots__ = ("w", "r", "name")

    def __init__(self, name=""):
        self.w = None
        self.r = {}
        self.name = name


class RR:
    def __init__(self, items):
        self.items = items
        self.i = 0

    def next(self):
        it = self.items[self.i % len(self.items)]
        self.i += 1
        return it


class KB:
    ENG = ("pe", "act", "dve", "pool", "sp")

    def __init__(self, nc, ndma=20, nsw=8):
        self.nc = nc
        self.e = {"pe": nc.tensor, "act": nc.scalar, "dve": nc.vector, "pool": nc.gpsimd, "sp": nc.sync}
        self.sem = {k: nc.alloc_semaphore(name="sem_" + k) for k in self.ENG}
        self.cnt = {k: 0 for k in self.ENG}
        self.seen = {k: {} for k in self.ENG}
        self.dsem = {"sp": [nc.alloc_semaphore(name="dsem%d" % i) for i in range(ndma)],
                     "pool": [nc.alloc_semaphore(name="wsem%d" % i) for i in range(nsw)],
                     "act": [nc.alloc_semaphore(name="asem%d" % i) for i in range(6)]}
        self.dval = {"sp": [0] * ndma, "pool": [0] * nsw, "act": [0] * 6}
        self.dnext = {"sp": 0, "pool": 0, "act": 0}
        self.pend = {k: [] for k in self.ENG}

    def _need(self, e, key, val):
        if self.seen[e].get(key, 0) >= val:
            return
        sem = self.sem[key] if isinstance(key, str) else self.dsem[key[1]][key[2]]
        self.e[e].wait_ge(sem, val)
        self.seen[e][key] = val

    def _dep1(self, e, key, val):
        if key == "pe" and e == "pe":
            return
        self._need(e, key, val)

    def _deps(self, e, reads, writes):
        for t in reads:
            if t.w is not None:
                self._dep1(e, t.w[0], t.w[1])
        for t in writes:
            if t.w is not None:
                self._dep1(e, t.w[0], t.w[1])
            for k, v in t.r.items():
                self._dep1(e, k, v)

    def op(self, e, fn, reads=(), writes=(), sig=True):
        self._deps(e, reads, writes)
        inst = fn(self.e[e])
        if sig:
            self.cnt[e] += 1
            idx = self.cnt[e]
            inst.then_inc(self.sem[e], 1)
            for (rr, ww) in self.pend[e] + [(reads, writes)]:
                for t in ww:
                    t.w = (e, idx)
                    t.r = {}
                for t in rr:
                    if t.r.get(e, 0) < idx:
                        t.r[e] = idx
            self.pend[e] = []
        else:
            self.pend[e].append((tuple(reads), tuple(writes)))
        return inst

    def dma(self, q, out, in_, reads=(), writes=(), **kw):
        i = self.dnext[q]
        self.dnext[q] = (i + 1) % len(self.dsem[q])
        key = ("d", q, i)
        if self.dval[q][i] > 0:
            self._need(q, key, self.dval[q][i])
        self._deps(q, reads, writes)
        inst = self.e[q].dma_start(out=out, in_=in_, **kw)
        self.dval[q][i] += 16
        inst.then_inc(self.dsem[q][i], 16)
        for t in writes:
            t.w = (key, self.dval[q][i])
            t.r = {}
        for t in reads:
            t.r[key] = self.dval[q][i]

    def _alld(self):
        for q in ("sp", "pool", "act"):
            for i in range(len(self.dsem[q])):
                if self.dval[q][i] > 0:
                    yield ("d", q, i), self.dval[q][i]

    def barrier(self):
        for e in self.ENG:
            assert not self.pend[e]
        for e in self.ENG:
            for f in self.ENG:
                if f != e and self.cnt[f] > 0:
                    self._need(e, f, self.cnt[f])
            for key, val in self._alld():
                self._need(e, key, val)

    def finish(self):
        for key, val in self._alld():
            self._need("sp", key, val)


def build(debug=False):
    nc = bass.Bass("TRN2", target_bir_lowering=False)
    kb = KB(nc)

    def din(name, shape, dt=F32):
        return nc.dram_tensor(name, list(shape), dt, kind="ExternalInput").ap()

    def dscr(name, shape, dt):
        return nc.dram_tensor(name, list(shape), dt, kind=("ExternalOutput" if (debug and name in DEBUG_KEEP) else "Internal")).ap()

    xbT = din("xbT", [1024, 8192]); xsT = din("xsT", [1024, 6144]); pT = din("pT", [256, 4096])
    posb = din("posb", [32, 8192], I32); poso = din("poso", [32, 4096], I32)
    hfl = din("hfl", [16, 128, 128])
    cst = din("cst", [128, 64]); cmat = din("cmat", [128, 640]); bsel = din("bsel", [32, 128]); relb = din("relb", [32, 16])
    cmk = din("cmk", [128, 4096])
    w2a = din("w2a", [1024, 3456]); woa = din("woa", [1024, 1024]); w2b = din("w2b", [1024, 256])
    wuq = din("wuq", [256, 1536]); wuqs = din("wuqs", [256, 1536]); w1 = din("w1", [1024, 320])
    wuk = din("wuk", [128, 1024]); wuv = din("wuv", [128, 1024])
    w4 = din("w4", [1024, 2048]); wob = din("wob", [1024, 1024]); wout = din("wout", [1024, 1024])
    wpg = din("wpg", [1024, 1024]); wpp = din("wpp", [256, 1024])
    outT = nc.dram_tensor("outT", [1024, 4096], F32, kind="ExternalOutput").ap()
    MAd = dscr("MAd", [1024, 4096], F32)
    Qd = dscr("Qd", [16, 96, 4096], BF16)
    Kd = dscr("Kd", [1024, 8192], BF16)
    Vd = dscr("Vd", [16, 16, 128, 512], BF16)
    OBd = dscr("OBd", [1024, 4096], F32)
    EBv = dscr("EBv", [16, 384], F32)

    def wview(w):
        return w.rearrange("(c p) n -> p c n", p=128)

    with ExitStack() as g:
        def sb(name, shape, dt, es=g):
            return es.enter_context(nc.sbuf_tensor(name, list(shape), dt))

        def pst(name, shape, es):
            return es.enter_context(nc.psum_tensor(name, list(shape), F32))

        POSIg = [(sb("POSIg%d" % i, [96, 512], I32), Trk()) for i in range(2)]
        CT = sb("CT", [128, 64], F32); CTt = Trk()
        CMAT = sb("CMAT", [128, 640], BF16); CMATt = Trk()
        kb.dma("sp", CT[:, :], cst[:, :], writes=[CTt])
        kb.dma("pool", CMAT[:, :], cmat[:, :], writes=[CMATt])
        ONES = CMAT[:, 0:128]; BLK64 = CMAT[:, 128:256]; ONE96 = CMAT[0:96, 256:352]; KR32 = CMAT[0:96, 384:480]; JFLIP = CMAT[:, 512:640]

        def gn(col, rows=slice(0, 128)):
            return CT[rows, col:col + 1]

        def norm1024(X, Xt, H, Htl, nT, SQ, SQt, RS, RSt, gcol0, nb):
            kb.op("act", lambda e: e.activation(out=SQ[:, :, 0:nT], in_=X[:, :, 0:nT], func=AF.Square), reads=[Xt, CTt], writes=[SQt])
            ps, pt = nb()
            for c in range(8):
                kb.op("pe", lambda e: e.matmul(ps[:, 0:nT], lhsT=ONES, rhs=SQ[:, c, 0:nT], start=(c == 0), stop=(c == 7)),
                      reads=[SQt, CMATt], writes=[pt], sig=(c == 7))
            kb.op("act", lambda e: e.activation(out=RS[:, 0:nT], in_=ps[:, 0:nT], func=AF.Ln, scale=1.0 / 1024, bias=gn(C_EPS)),
                  reads=[pt, CTt], writes=[RSt])
            kb.op("act", lambda e: e.activation(out=RS[:, 0:nT], in_=RS[:, 0:nT], func=AF.Exp, scale=-0.5), reads=[RSt], writes=[RSt])
            for c in range(8):
                kb.op("dve", lambda e: e.scalar_tensor_tensor(out=H[:, c, 0:nT], in0=X[:, c, 0:nT], scalar=gn(gcol0 + c), in1=RS[:, 0:nT],
                                                              op0=ALU.mult, op1=ALU.mult), reads=[Xt, RSt, CTt], writes=[Htl[c]])

        def rstd_from(z, zt, rows, n, lhsT, inv_n, sqp, rsp, nb):
            sq, sqt = sqp.next()
            kb.op("act", lambda e: e.activation(out=sq[rows, 0:n], in_=z[rows, 0:n], func=AF.Square), reads=[zt], writes=[sqt])
            ps2, p2t = nb()
            kb.op("pe", lambda e: e.matmul(ps2[rows, 0:n], lhsT=lhsT, rhs=sq[rows, 0:n], start=True, stop=True), reads=[sqt, CMATt], writes=[p2t])
            rs, rst = rsp.next()
            kb.op("act", lambda e: e.activation(out=rs[rows, 0:n], in_=ps2[rows, 0:n], func=AF.Ln, scale=inv_n, bias=gn(C_EPS, rows)),
                  reads=[p2t, CTt], writes=[rst])
            kb.op("act", lambda e: e.activation(out=rs[rows, 0:n], in_=rs[rows, 0:n], func=AF.Exp, scale=-0.5), reads=[rst], writes=[rst])
            return rs, rst

        def rstd_gen(z, zt, rows, n, lhsT, inv_n, sqp, rsp, nb, out):
            sq, sqt = sqp.next(); ps2, p2t = nb(); rs, rst = rsp.next()
            kb.op("act", lambda e: e.activation(out=sq[rows, 0:n], in_=z[rows, 0:n], func=AF.Square), reads=[zt], writes=[sqt])
            yield
            kb.op("pe", lambda e: e.matmul(ps2[rows, 0:n], lhsT=lhsT, rhs=sq[rows, 0:n], start=True, stop=True), reads=[sqt, CMATt], writes=[p2t])
            yield
            kb.op("act", lambda e: e.activation(out=rs[rows, 0:n], in_=ps2[rows, 0:n], func=AF.Ln, scale=inv_n, bias=gn(C_EPS, rows)),
                  reads=[p2t, CTt], writes=[rst])
            yield
            kb.op("act", lambda e: e.activation(out=rs[rows, 0:n], in_=rs[rows, 0:n], func=AF.Exp, scale=-0.5), reads=[rst], writes=[rst])
            yield
            out.append((rs, rst))

        def rope_tables(POSI, POSIt, n, ANG, KK, SN, CO, tt):
            R = slice(64, 96)
            kb.op("dve", lambda e: e.tensor_copy(out=ANG[R, 0:n], in_=POSI[R, 0:n]), reads=[POSIt], writes=[tt["ang"]])
            kb.op("dve", lambda e: e.tensor_scalar(out=ANG[R, 0:n], in0=ANG[R, 0:n], scalar1=gn(C_INVF, R), scalar2=None, op0=ALU.mult),
                  reads=[tt["ang"], CTt], writes=[tt["ang"]])
            kb.op("dve", lambda e: e.tensor_scalar(out=KK[R, 0:n], in0=ANG[R, 0:n], scalar1=1.0 / TWO_PI, scalar2=MAGIC, op0=ALU.mult, op1=ALU.add),
                  reads=[tt["ang"]], writes=[tt["kk"]])
            kb.op("dve", lambda e: e.tensor_scalar(out=KK[R, 0:n], in0=KK[R, 0:n], scalar1=-MAGIC, scalar2=None, op0=ALU.add),
                  reads=[tt["kk"]], writes=[tt["kk"]])
            for cw in (CW1, CW2, CW3):
                kb.op("dve", lambda e: e.scalar_tensor_tensor(out=ANG[R, 0:n], in0=KK[R, 0:n], scalar=-cw, in1=ANG[R, 0:n], op0=ALU.mult, op1=ALU.add),
                      reads=[tt["kk"], tt["ang"]], writes=[tt["ang"]])
            kb.op("dve", lambda e: e.tensor_scalar(out=KK[R, 0:n], in0=ANG[R, 0:n], scalar1=PI_IN, scalar2=None, op0=ALU.is_gt),
                  reads=[tt["ang"], tt["kk"]], writes=[tt["kk"]])
            kb.op("dve", lambda e: e.scalar_tensor_tensor(out=ANG[R, 0:n], in0=KK[R, 0:n], scalar=-TWO_PI, in1=ANG[R, 0:n], op0=ALU.mult, op1=ALU.add),
                  reads=[tt["kk"], tt["ang"]], writes=[tt["ang"]])
            kb.op("dve", lambda e: e.tensor_scalar(out=KK[R, 0:n], in0=ANG[R, 0:n], scalar1=-PI_IN, scalar2=None, op0=ALU.is_lt),
                  reads=[tt["ang"], tt["kk"]], writes=[tt["kk"]])
            kb.op("dve", lambda e: e.scalar_tensor_tensor(out=ANG[R, 0:n], in0=KK[R, 0:n], scalar=TWO_PI, in1=ANG[R, 0:n], op0=ALU.mult, op1=ALU.add),
                  reads=[tt["kk"], tt["ang"]], writes=[tt["ang"]])
            kb.op("dve", lambda e: e.tensor_scalar(out=ANG[R, 0:n], in0=ANG[R, 0:n], scalar1=PI_IN, scalar2=-PI_IN, op0=ALU.min, op1=ALU.max),
                  reads=[tt["ang"]], writes=[tt["ang"]])
            kb.op("act", lambda e: e.activation(out=SN[R, 0:n], in_=ANG[R, 0:n], func=AF.Sin), reads=[tt["ang"]], writes=[tt["sn"]])
            kb.op("dve", lambda e: e.tensor_scalar(out=SN[R, 0:n], in0=SN[R, 0:n], scalar1=gn(C_SSGN, R), scalar2=None, op0=ALU.mult),
                  reads=[tt["sn"], CTt], writes=[tt["sn"]])
            kb.op("act", lambda e: e.activation(out=KK[R, 0:n], in_=ANG[R, 0:n], func=AF.Abs),
                  reads=[tt["ang"]], writes=[tt["kk"]])
            kb.op("act", lambda e: e.activation(out=CO[R, 0:n], in_=KK[R, 0:n], func=AF.Sin, scale=-1.0, bias=gn(C_HPI, R)),
                  reads=[tt["kk"], CTt], writes=[tt["co"]])

        for ph in _phase("2A"):
            def s2(name, shape, dt):
                return sb("a_" + name, shape, dt, ph)
            W2A = s2("W2A", [128, 8, 3456], BF16); W2At = Trk()
            WOA = s2("WOA", [128, 8, 1024], BF16); WOAt = Trk()
            kb.dma("pool", W2A[:, :, 0:1408], wview(w2a)[:, :, 0:1408], writes=[W2At])
            kb.dma("pool", W2A[:, :, 1408:3456], wview(w2a)[:, :, 1408:3456], writes=[W2At])
            kb.dma("pool", WOA[:, :, :], wview(woa), writes=[WOAt])
            EB = s2("EB", [128, 16, 2, 128], BF16); EBt = Trk()
            SKROW = s2("SKROW", [1, 16, 2, 128], BF16); skt = Trk()
            ESKR = s2("ESKR", [1, 16], F32); eskt = Trk()
            ONER = s2("ONER", [1, 128], BF16); onert = Trk()
            with ExitStack() as st:
                def s3(name, shape, dt):
                    return sb("a_" + name, shape, dt, st)
                BSEL = s3("BSEL", [32, 128], F32); RELB = s3("RELB", [32, 16], F32); bt_ = Trk(); rt_ = Trk()
                kb.dma("sp", BSEL[:, :], bsel[:, :], writes=[bt_]); kb.dma("sp", RELB[:, :], relb[:, :], writes=[rt_])
                ZR = s3("ZR", [16, 384], F32); zrt = Trk()
                kb.op("dve", lambda e: e.memset(ZR[:, :], 0.0), writes=[zrt])
                EBt_d = Trk()
                kb.dma("sp", EBv[:, :], ZR[:, :], reads=[zrt], writes=[EBt_d])
                EFT = s3("EFT", [16, 128], F32); eft = Trk()
                with ExitStack() as pp:
                    psb = pst("a_psb", [128, 512], pp); psbt = Trk()
                    kb.op("pe", lambda e: e.matmul(psb[0:16, 0:128], lhsT=RELB[:, :], rhs=BSEL[:, :], start=True, stop=True), reads=[bt_, rt_], writes=[psbt])
                    kb.op("act", lambda e: e.activation(out=EFT[:, :], in_=psb[0:16, 0:128], func=AF.Exp), reads=[psbt], writes=[eft])
                kb.dma("sp", EBv[:, 127:255], EFT[:, :], reads=[eft], writes=[EBt_d])
                EBf = s3("EBf", [128, 16, 2, 128], BF16); EBft = Trk()
                for pv in range(2):
                    src = bass.AP(tensor=EBv.tensor, offset=(128 if pv == 0 else 0), ap=[[1, 128], [384, 16], [1, 128]])
                    kb.dma("pool", EBf[:, :, pv, :], src, reads=[EBt_d], writes=[EBft])
                with ExitStack() as pp:
                    psj = [(pst("a_psj%d" % i, [128, 512], pp), Trk()) for i in range(2)]
                    EBfl = EBf[:, :, :, :].rearrange("p h v q -> p (h v q)")
                    EBl = EB[:, :, :, :].rearrange("p h v q -> p (h v q)")
                    for i in range(8):
                        pj, pjt = psj[i % 2]
                        kb.op("pe", lambda e: e.matmul(pj[:, :], lhsT=JFLIP, rhs=EBfl[:, i * 512:(i + 1) * 512], start=True, stop=True), reads=[EBft, CMATt], writes=[pjt])
                        kb.op("dve", lambda e: e.tensor_copy(out=EBl[:, i * 512:(i + 1) * 512], in_=pj[:, :]), reads=[pjt], writes=[EBt])
                kb.barrier()
            kb.op("act", lambda e: e.activation(out=ESKR[0:1, :], in_=CT[0:1, C_SINK:C_SINK + 16], func=AF.Exp), reads=[CTt], writes=[eskt])
            kb.op("dve", lambda e: e.memset(SKROW[:, :, :, :], 0.0), writes=[skt])
            kb.op("dve", lambda e: e.memset(ONER[:, :], 1.0), writes=[onert])
            for h in range(16):
                for par in range(2):
                    sc = slice(64, 128) if par == 0 else slice(0, 64)
                    kb.op("dve", lambda e: e.tensor_scalar(out=SKROW[0:1, h, par, sc], in0=SKROW[0:1, h, par, sc], scalar1=ESKR[0:1, h:h + 1],
                                                           scalar2=None, op0=ALU.add), reads=[skt, eskt], writes=[skt])
            X = [s2("X%d" % i, [128, 8, 384], F32) for i in range(2)]; Xt = [Trk(), Trk()]
            SQ = s2("SQ", [128, 8, 384], BF16); SQt = Trk()
            RSs = [(s2("RSb%d" % i, [128, 384], F32), Trk()) for i in range(2)]
            Hs = [(s2("H%d" % i, [128, 8, 384], BF16), [Trk() for _ in range(8)]) for i in range(2)]
            KANs = [([s2("KAN%d_%d" % (k, i), [128, 384], BF16) for i in range(2)], [Trk(), Trk()]) for k in range(2)]
            VAs = [(s2("VA%d" % i, [128, 3, 2, 2, 128], BF16), Trk(), Trk()) for i in range(2)]
            QANs = [(s2("QAN%d" % i, [128, 8, 256], BF16), [Trk() for _ in range(8)]) for i in range(2)]
            SZAs = [(s2("SZA%d" % i, [128, 8, 256], F32), [Trk() for _ in range(8)]) for i in range(2)]
            YAs = [(s2("YA%d" % i, [128, 8, 256], BF16), [Trk() for _ in range(8)]) for i in range(2)]
            sqp = RR([(s2("sq%d" % i, [128, 384], BF16), Trk()) for i in range(4)])
            rsp = RR([(s2("rs%d" % i, [128, 384], F32), Trk()) for i in range(4)])
            Ep = RR([(s2("E%d" % i, [128, 4, 2, 128], BF16), Trk()) for i in range(3)])
            DENp = RR([(s2("DEN%d" % i, [128, 2, 128], F32), Trk()) for i in range(4)])
            TMPp = RR([(s2("TMP%d" % i, [128, 2, 128], F32), Trk()) for i in range(4)])
            SGp = RR([(s2("SG%d" % i, [128, 256], F32), Trk()) for i in range(2)])
            MAp = RR([(s2("MAo%d" % i, [128, 256], F32), Trk()) for i in range(3)])
            PSg = RR([(pst("a_ps%d" % i, [128, 512], ph), Trk()) for i in range(4)])
            PSs = RR([(pst("a_pss%d" % i, [128, 4, 2, 128], ph), Trk()) for i in range(2)])
            nb = PSg.next
            for i in range(2):
                kb.op("pool", lambda e: e.memset(VAs[i][0][:, :, :, :, :], 1.0), writes=[VAs[i][1], VAs[i][2]])
            xs3 = wview(xsT)

            def body_a(ch):
                Xc, Xct = X[ch % 2], Xt[ch % 2]
                RS, RSt = RSs[ch % 2]; H, Ht = Hs[ch % 2]; KAN, KANt = KANs[ch % 2]; VA, VAt, VAf = VAs[ch % 2]
                QAN, QANt = QANs[ch % 2]; SZA, SZAt = SZAs[ch % 2]; YA, YAt = YAs[ch % 2]
                if ch == 0:
                    kb.dma("act", X[0][:, :, :], xs3[:, :, 0:384], writes=[Xt[0]])
                if ch + 1 < LIM["a_chunks"]:
                    kb.dma("act", X[(ch + 1) % 2][:, :, :], xs3[:, :, (ch + 1) * 384:(ch + 2) * 384], writes=[Xt[(ch + 1) % 2]])
                norm1024(Xc, Xct, H, Ht, 384, SQ, SQt, RS, RSt, C_GN, nb)
                yield
                if LIM["a_stage"] < 2:
                    return
                kb.dma("pool", VA[:, 0, :, 0, 64:128], hfl[ch].rearrange("t (g e) -> t g e", g=2), reads=[], writes=[VAf])
                kb.dma("pool", VA[:, 0, :, 1, 0:64], hfl[ch].rearrange("t (g e) -> t g e", g=2), reads=[], writes=[VAf])
                def ch_ka(gk):
                    ps, pt = nb()
                    for c in range(8):
                        kb.op("pe", lambda e: e.matmul(ps[:, 0:384], lhsT=W2A[:, c, 1024 + gk * 128:1152 + gk * 128], rhs=H[:, c, :], start=(c == 0), stop=(c == 7)),
                              reads=[W2At, Ht[c]], writes=[pt], sig=(c == 7))
                    yield
                    o = []
                    yield from rstd_gen(ps, pt, slice(0, 128), 384, BLK64, 1.0 / 64, sqp, rsp, nb, o)
                    rs, rst = o[0]
                    kb.op("dve", lambda e: e.scalar_tensor_tensor(out=KAN[gk][:, :], in0=ps[:, 0:384], scalar=gn(C_GAK), in1=rs[:, 0:384], op0=ALU.mult, op1=ALU.mult),
                          reads=[pt, rst, CTt], writes=[KANt[gk]])
                wave([ch_ka(0), ch_ka(1)])
                yield
                if LIM["a_stage"] < 3:
                    return

                def ch_va(blk):
                    ps, pt = nb()
                    for c in range(8):
                        kb.op("pe", lambda e: e.matmul(ps[:, 0:128], lhsT=H[:, c, blk * 128:(blk + 1) * 128], rhs=W2A[:, c, 1280:1408], start=(c == 0), stop=(c == 7)),
                              reads=[W2At, Ht[c]], writes=[pt], sig=(c == 7))
                    yield
                    pv3 = ps[:, 0:128].rearrange("p (g e) -> p g e", g=2)
                    kb.op("dve", lambda e: e.tensor_copy(out=VA[:, blk, :, 0, 0:64], in_=pv3), reads=[pt], writes=[VAt])
                    yield
                    kb.op("dve", lambda e: e.tensor_copy(out=VA[:, blk, :, 1, 64:128], in_=pv3), reads=[pt], writes=[VAt])
                wave([ch_va(blk) for blk in range(3)])
                yield
                if LIM["a_stage"] < 4:
                    return

                def ch_qa(cq):
                    ps, pt = nb()
                    for c in range(8):
                        kb.op("pe", lambda e: e.matmul(ps[:, 0:256], lhsT=W2A[:, c, cq * 128:(cq + 1) * 128], rhs=H[:, c, 128:384], start=(c == 0), stop=(c == 7)),
                              reads=[W2At, Ht[c]], writes=[pt], sig=(c == 7))
                    yield
                    o = []
                    yield from rstd_gen(ps, pt, slice(0, 128), 256, BLK64, 1.0 / 64, sqp, rsp, nb, o)
                    rs, rst = o[0]
                    kb.op("dve", lambda e: e.scalar_tensor_tensor(out=QAN[:, cq, :], in0=ps[:, 0:256], scalar=gn(C_GAQ), in1=rs[:, 0:256], op0=ALU.mult, op1=ALU.mult),
                          reads=[pt, rst, CTt], writes=[QANt[cq]])
                for w in range(4):
                    wave([ch_qa(2 * w), ch_qa(2 * w + 1)])
                    yield

                def ch_za(cq):
                    ps, pt = nb()
                    for c in range(8):
                        kb.op("pe", lambda e: e.matmul(ps[:, 0:256], lhsT=W2A[:, c, 1408 + cq * 128:1536 + cq * 128], rhs=H[:, c, 128:384], start=(c == 0), stop=(c == 7)),
                              reads=[W2At, Ht[c]], writes=[pt], sig=(c == 7))
                    yield
                    kb.op("act", lambda e: e.activation(out=SZA[:, cq, :], in_=ps[:, 0:256], func=AF.Silu), reads=[pt], writes=[SZAt[cq]])
                for w in range(2):
                    wave([ch_za(cq) for cq in range(4 * w, 4 * w + 4)])
                yield
                if LIM["a_stage"] < 5:
                    return

                def ch_at(ob, hg):
                    qs = slice((ob - 1) * 128, ob * 128)
                    S, St = PSs.next()
                    O, Ot = nb()
                    E, Et = Ep.next()
                    gk = hg // 2
                    for hh in range(4):
                        h = hg * 4 + hh; par = h % 2; cq = h // 2
                        pr = slice(par * 64, par * 64 + 64)
                        for pv in range(2):
                            kblk = ob - 1 + pv
                            last = (hh == 3 and pv == 1)
                            kb.op("pe", lambda e: e.matmul(S[:, par * 2 + hh // 2, pv, :], lhsT=KAN[gk][pr, kblk * 128:(kblk + 1) * 128], rhs=QAN[pr, cq, qs], start=True, stop=True),
                                  reads=[KANt[gk], QANt[cq]], writes=[St], sig=last)
                    yield
                    kb.op("act", lambda e: e.activation(out=E[:, :, :, :], in_=S[:, :, :, :], func=AF.Exp, scale=0.125), reads=[St], writes=[Et])
                    yield
                    kb.op("dve", lambda e: e.tensor_tensor(out=E[:, :, :, :], in0=E[:, :, :, :], in1=EB[:, hg * 4:hg * 4 + 4, :, :], op=ALU.mult),
                          reads=[Et, EBt], writes=[Et])
                    yield
                    for hh in range(4):
                        h = hg * 4 + hh; par = h % 2
                        kb.op("pe", lambda e: e.matmul(O[:, hh * 128:(hh + 1) * 128], lhsT=SKROW[0:1, h, par, :], rhs=ONER[0:1, :], start=True, stop=False),
                              reads=[skt, onert], writes=[Ot], sig=False)
                        for pv in range(2):
                            kblk = ob - 1 + pv
                            kb.op("pe", lambda e: e.matmul(O[:, hh * 128:(hh + 1) * 128], lhsT=VA[:, kblk, gk, par, :], rhs=E[:, par * 2 + hh // 2, pv, :], start=False, stop=(pv == 1)),
                                  reads=[VAt, VAf, Et], writes=[Ot], sig=(hh == 3 and pv == 1))
                    yield
                    O3 = O[:, :].rearrange("p (h q) -> p h q", h=4)
                    c0 = hg * 2
                    dd = []
                    for par in range(2):
                        osl = slice(par * 64, par * 64 + 64); ssl = slice((1 - par) * 64, (1 - par) * 64 + 64)
                        DEN, DENt = DENp.next(); TMP, TMPt = TMPp.next()
                        dd.append((osl, ssl, DEN, DENt, TMP, TMPt))
                    for par in range(2):
                        osl, ssl, DEN, DENt, TMP, TMPt = dd[par]
                        kb.op("dve", lambda e: e.reciprocal(out=DEN[osl, :, :], in_=O3[ssl, par::2, :]), reads=[Ot], writes=[DENt])
                        yield
                    for par in range(2):
                        osl, ssl, DEN, DENt, TMP, TMPt = dd[par]
                        kb.op("dve", lambda e: e.tensor_tensor(out=TMP[osl, :, :], in0=O3[osl, par::2, :], in1=DEN[osl, :, :], op=ALU.mult),
                              reads=[Ot, DENt], writes=[TMPt])
                        yield
                    for par in range(2):
                        osl, ssl, DEN, DENt, TMP, TMPt = dd[par]
                        kb.op("dve", lambda e: e.tensor_tensor(out=YA[osl, c0:c0 + 2, qs], in0=TMP[osl, :, :], in1=SZA[osl, c0:c0 + 2, qs], op=ALU.mult),
                              reads=[TMPt, SZAt[c0], SZAt[c0 + 1]], writes=[YAt[c0], YAt[c0 + 1]])
                        yield
                for ob in (1, 2):
                    for w in range(2):
                        wave([ch_at(ob, 2 * w), ch_at(ob, 2 * w + 1)])
                        yield
                if LIM["a_stage"] < 6:
                    return

                def ch_out(oc):
                    py, pyt = nb(); pg, pgt = nb()
                    for c in range(8):
                        kb.op("pe", lambda e: e.matmul(py[:, 0:256], lhsT=WOA[:, c, oc * 128:(oc + 1) * 128], rhs=YA[:, c, :], start=(c == 0), stop=(c == 7)),
                              reads=[WOAt, YAt[c]], writes=[pyt], sig=(c == 7))
                    yield
                    for c in range(8):
                        kb.op("pe", lambda e: e.matmul(pg[:, 0:256], lhsT=W2A[:, c, 2432 + oc * 128:2560 + oc * 128], rhs=H[:, c, 128:384], start=(c == 0), stop=(c == 7)),
                              reads=[W2At, Ht[c]], writes=[pgt], sig=(c == 7))
                    yield
                    SG, SGt = SGp.next()
                    kb.op("act", lambda e: e.activation(out=SG[:, :], in_=pg[:, 0:256], func=AF.Sigmoid), reads=[pgt], writes=[SGt])
                    yield
                    MAo, MAot = MAp.next()
                    kb.op("dve", lambda e: e.tensor_tensor(out=MAo[:, :], in0=py[:, 0:256], in1=SG[:, :], op=ALU.mult), reads=[pyt, SGt], writes=[MAot])
                    yield
                    kb.dma("sp", MAd[oc * 128:(oc + 1) * 128, ch * 256:(ch + 1) * 256], MAo[:, :], reads=[MAot])
                for w in range(4):
                    wave([ch_out(2 * w), ch_out(2 * w + 1)])
                    if w == 1:
                        yield
                yield
            run_pipe([(lambda ch=ch: body_a(ch)) for ch in range(LIM["a_chunks"])], 7)
            kb.barrier()

        for ph in _phase("2B"):
            def s2(name, shape, dt):
                return sb("b_" + name, shape, dt, ph)
            W2B = s2("W2B", [128, 8, 256], BF16); W2Bt = Trk()
            WUQ = s2("WUQ", [128, 2, 1536], BF16); WUQt = Trk()
            WUQS = s2("WUQS", [128, 2, 1536], BF16); WUQSt = Trk()
            kb.dma("pool", W2B[:, :, :], wview(w2b), writes=[W2Bt])
            kb.dma("pool", WUQ[:, :, :], wview(wuq), writes=[WUQt])
            kb.dma("pool", WUQS[:, :, :], wview(wuqs), writes=[WUQSt])
            X = [s2("X%d" % i, [128, 8, 512], F32) for i in range(2)]; Xt = [Trk(), Trk()]
            SQ = s2("SQ", [128, 8, 512], BF16); SQt = Trk()
            RSs = [(s2("RS%d" % i, [128, 512], F32), Trk()) for i in range(2)]
            Hs = [(s2("H%d" % i, [128, 8, 512], BF16), [Trk() for _ in range(8)]) for i in range(2)]
            CQNs = [(s2("CQN%d" % i, [128, 2, 512], BF16), Trk()) for i in range(2)]
            POSIs = POSIg
            TBs = [dict(ANG=s2("ANG%d" % i, [96, 512], F32), KK=s2("KK%d" % i, [96, 512], F32), SN=s2("SN%d" % i, [96, 512], F32), CO=s2("CO%d" % i, [96, 512], F32),
                        tt={k: Trk() for k in ("ang", "kk", "sn", "co")}) for i in range(2)]
            sqp = RR([(s2("sq%d" % i, [128, 512], BF16), Trk()) for i in range(4)])
            rsp = RR([(s2("rs%d" % i, [128, 512], F32), Trk()) for i in range(4)])
            QNr = RR([(s2("QNr%d" % i, [96, 512], F32), Trk()) for i in range(3)])
            QSr = RR([(s2("QSr%d" % i, [96, 512], F32), Trk()) for i in range(3)])
            Qo = RR([(s2("Qo%d" % i, [96, 512], BF16), Trk()) for i in range(4)])
            PSg = RR([(pst("b_ps%d" % i, [128, 512], ph), Trk()) for i in range(8)])
            nb = PSg.next
            xs4 = xsT.rearrange("(c p) (k t) -> p c k t", p=128, t=384)

            def loadx(j, Xd, Xdt):
                for u in range(2):
                    kb.dma("act", Xd[:, :, u * 256:(u + 1) * 256], xs4[:, :, 2 * j + u, 128:384], writes=[Xdt])
            def body_b(j):
                Xc, Xct = X[j % 2], Xt[j % 2]
                RS, RSt = RSs[j % 2]; H, Ht = Hs[j % 2]; CQN, CQNt = CQNs[j % 2]; POSI, POSIt = POSIs[j % 2]
                TB = TBs[j % 2]; ANG, KK, SN, CO, tt = TB["ANG"], TB["KK"], TB["SN"], TB["CO"], TB["tt"]
                if j == 0:
                    loadx(0, X[0], Xt[0])
                    kb.dma("act", POSIs[0][0][64:96, :], poso[:, 0:512], writes=[POSIs[0][1]])
                if j + 1 < 8:
                    loadx(j + 1, X[(j + 1) % 2], Xt[(j + 1) % 2])
                    kb.dma("act", POSIs[(j + 1) % 2][0][64:96, :], poso[:, (j + 1) * 512:(j + 2) * 512], writes=[POSIs[(j + 1) % 2][1]])
                norm1024(Xc, Xct, H, Ht, 512, SQ, SQt, RS, RSt, C_GN, nb)
                yield
                rope_tables(POSI, POSIt, 512, ANG, KK, SN, CO, tt)
                COg, SNg = TB["KK"], TB["ANG"]
                kb.op("dve", lambda e: e.tensor_scalar(out=COg[64:96, :], in0=CO[64:96, :], scalar1=gn(C_GBQ, slice(64, 96)), scalar2=None, op0=ALU.mult),
                      reads=[tt["co"], tt["kk"], CTt], writes=[tt["kk"]])
                kb.op("dve", lambda e: e.tensor_scalar(out=SNg[64:96, :], in0=SN[64:96, :], scalar1=gn(C_GBQS, slice(64, 96)), scalar2=None, op0=ALU.mult),
                      reads=[tt["sn"], tt["ang"], CTt], writes=[tt["ang"]])
                yield
                pz = []
                for cc in range(2):
                    ps, pt = nb()
                    for c in range(8):
                        kb.op("pe", lambda e: e.matmul(ps[:, :], lhsT=W2B[:, c, cc * 128:(cc + 1) * 128], rhs=H[:, c, :], start=(c == 0), stop=(c == 7)),
                              reads=[W2Bt, Ht[c]], writes=[pt], sig=(c == 7))
                    pz.append((ps, pt))
                sqs = []
                for cc in range(2):
                    sq, sqt = sqp.next()
                    kb.op("act", lambda e: e.activation(out=sq[:, :], in_=pz[cc][0][:, :], func=AF.Square), reads=[pz[cc][1]], writes=[sqt])
                    sqs.append((sq, sqt))
                ps2, p2t = nb()
                for cc in range(2):
                    kb.op("pe", lambda e: e.matmul(ps2[:, :], lhsT=ONES, rhs=sqs[cc][0][:, :], start=(cc == 0), stop=(cc == 1)),
                          reads=[sqs[cc][1], CMATt], writes=[p2t], sig=(cc == 1))
                rs, rst = rsp.next()
                kb.op("act", lambda e: e.activation(out=rs[:, :], in_=ps2[:, :], func=AF.Ln, scale=1.0 / 256, bias=gn(C_EPS)), reads=[p2t, CTt], writes=[rst])
                kb.op("act", lambda e: e.activation(out=rs[:, :], in_=rs[:, :], func=AF.Exp, scale=-0.5), reads=[rst], writes=[rst])
                for cc in range(2):
                    kb.op("dve", lambda e: e.scalar_tensor_tensor(out=CQN[:, cc, :], in0=pz[cc][0][:, :], scalar=gn(C_GCQ + cc), in1=rs[:, :], op0=ALU.mult, op1=ALU.mult),
                          reads=[pz[cc][1], rst, CTt], writes=[CQNt])
                yield
                R = slice(64, 96)
                def ch_q(h):
                    pq, pqt = nb(); pw, pwt = nb()
                    for cc in range(2):
                        kb.op("pe", lambda e: e.matmul(pq[0:96, :], lhsT=WUQ[:, cc, h * 96:(h + 1) * 96], rhs=CQN[:, cc, :], start=(cc == 0), stop=(cc == 1)),
                              reads=[WUQt, CQNt], writes=[pqt], sig=(cc == 1))
                    for cc in range(2):
                        kb.op("pe", lambda e: e.matmul(pw[0:96, :], lhsT=WUQS[:, cc, h * 96:(h + 1) * 96], rhs=CQN[:, cc, :], start=(cc == 0), stop=(cc == 1)),
                              reads=[WUQSt, CQNt], writes=[pwt], sig=(cc == 1))
                    yield
                    o = []
                    yield from rstd_gen(pq, pqt, slice(0, 96), 512, ONE96, 1.0 / 96, sqp, rsp, nb, o)
                    rs, rst = o[0]
                    qo, qot = Qo.next(); qn, qnt = QNr.next(); qw, qwt = QSr.next()
                    kb.op("dve", lambda e: e.scalar_tensor_tensor(out=qo[0:64, :], in0=pq[0:64, :], scalar=gn(C_GBQ, slice(0, 64)), in1=rs[0:64, :], op0=ALU.mult, op1=ALU.mult),
                          reads=[pqt, rst, CTt], writes=[qot])
                    yield
                    kb.op("dve", lambda e: e.tensor_tensor(out=qn[R, :], in0=pq[R, :], in1=COg[R, :], op=ALU.mult), reads=[pqt, tt["kk"]], writes=[qnt])
                    yield
                    kb.op("dve", lambda e: e.tensor_tensor(out=qw[R, :], in0=pw[R, :], in1=SNg[R, :], op=ALU.mult), reads=[pwt, tt["ang"]], writes=[qwt])
                    yield
                    kb.op("dve", lambda e: e.tensor_tensor(out=qn[R, :], in0=qn[R, :], in1=qw[R, :], op=ALU.add), reads=[qnt, qwt], writes=[qnt])
                    yield
                    kb.op("dve", lambda e: e.tensor_tensor(out=qo[R, :], in0=qn[R, :], in1=rs[R, :], op=ALU.mult), reads=[qnt, rst], writes=[qot])
                    yield
                    kb.dma("sp", Qd[h, :, j * 512:(j + 1) * 512], qo[:, :], reads=[qot])
                for w in range(8):
                    wave([ch_q(2 * w), ch_q(2 * w + 1)])
                    yield
            run_pipe([(lambda j=j: body_b(j)) for j in range(8)], 4)
            kb.barrier()

        kvs = ExitStack()
        Kbuf = [sb("Kbuf%d" % i, [96, 8192], BF16, kvs) for i in range(2)]
        Kbt_n = [Trk(), Trk()]
        Kbt_r = [Trk(), Trk()]
        for ph in _phase("1"):
            def s2(name, shape, dt):
                return sb("c_" + name, shape, dt, ph)
            W1 = s2("W1", [128, 8, 320], BF16); W1t = Trk()
            WUK = s2("WUK", [128, 1024], BF16); WUKt = Trk()
            WUV = s2("WUV", [128, 1024], BF16); WUVt = Trk()
            kb.dma("pool", W1[:, :, :], wview(w1), writes=[W1t])
            kb.dma("pool", WUK[:, :], wuk[:, :], writes=[WUKt])
            kb.dma("pool", WUV[:, :], wuv[:, :], writes=[WUVt])
            X = [s2("X%d" % i, [128, 8, 512], F32) for i in range(2)]; Xt = [Trk(), Trk()]
            SQ = s2("SQ", [128, 8, 512], BF16); SQt = Trk()
            RSs = [(s2("RS%d" % i, [128, 512], F32), Trk()) for i in range(2)]
            Hs = [(s2("H%d" % i, [128, 8, 512], BF16), [Trk() for _ in range(8)]) for i in range(2)]
            CKVp = RR([(s2("CKV%d" % i, [128, 512], BF16), Trk()) for i in range(2)])
            POSIs = POSIg
            TBs = [dict(ANG=s2("ANG%d" % i, [96, 512], F32), KK=s2("KK%d" % i, [96, 512], F32), SN=s2("SN%d" % i, [96, 512], F32), CO=s2("CO%d" % i, [96, 512], F32),
                        tt={k: Trk() for k in ("ang", "kk", "sn", "co")}) for i in range(2)]
            KRs = [(s2("KRN%d" % i, [96, 512], F32), s2("KRS%d" % i, [96, 512], F32), Trk(), Trk()) for i in range(2)]
            sqp = RR([(s2("sq%d" % i, [128, 512], BF16), Trk()) for i in range(4)])
            rsp = RR([(s2("rs%d" % i, [128, 512], F32), Trk()) for i in range(4)])
            KNp = RR([(s2("KN%d" % i, [128, 512], BF16), Trk()) for i in range(4)])
            VST = [s2("VST%d" % i, [128, 16, 4, 128], BF16) for i in range(2)]; VSTt = [Trk(), Trk()]
            PSg = RR([(pst("c_ps%d" % i, [128, 512], ph), Trk()) for i in range(8)])
            nb = PSg.next
            for i in range(2):
                kb.op("pool", lambda e: e.memset(VST[i][:, :, :, :], 1.0), writes=[VSTt[i]])
            xb3 = wview(xbT)
            Vd4 = Vd.rearrange("h b t e -> t h b e")
            R = slice(64, 96)

            def body_c(bt):
                ts = slice(bt * 512, (bt + 1) * 512)
                Xc, Xct = X[bt % 2], Xt[bt % 2]
                RS, RSt = RSs[bt % 2]; H, Ht = Hs[bt % 2]; POSI, POSIt = POSIs[bt % 2]
                TB = TBs[bt % 2]; ANG, KK, SN, CO, tt = TB["ANG"], TB["KK"], TB["SN"], TB["CO"], TB["tt"]
                KRN, KRS, krnt, krst = KRs[bt % 2]
                if bt == 0:
                    kb.dma("act", X[0][:, :, :], xb3[:, :, 0:512], writes=[Xt[0]])
                    kb.dma("act", POSIs[0][0][64:96, :], posb[:, 0:512], writes=[POSIs[0][1]])
                if bt + 1 < 16:
                    kb.dma("act", X[(bt + 1) % 2][:, :, :], xb3[:, :, (bt + 1) * 512:(bt + 2) * 512], writes=[Xt[(bt + 1) % 2]])
                    kb.dma("act", POSIs[(bt + 1) % 2][0][64:96, :], posb[:, (bt + 1) * 512:(bt + 2) * 512], writes=[POSIs[(bt + 1) % 2][1]])
                norm1024(Xc, Xct, H, Ht, 512, SQ, SQt, RS, RSt, C_GN, nb)
                yield
                rope_tables(POSI, POSIt, 512, ANG, KK, SN, CO, tt)
                yield
                pc, pct = nb(); pr_, prt = nb(); pw, pwt = nb()
                for (pp_, ppt, c0, m) in ((pc, pct, 0, 128), (pr_, prt, 128, 96), (pw, pwt, 224, 96)):
                    for c in range(8):
                        kb.op("pe", lambda e: e.matmul(pp_[0:m, :], lhsT=W1[:, c, c0:c0 + m], rhs=H[:, c, :], start=(c == 0), stop=(c == 7)),
                              reads=[W1t, Ht[c]], writes=[ppt], sig=(c == 7))
                CKV, CKVt = CKVp.next()

                def ch_ckv():
                    o = []
                    yield from rstd_gen(pc, pct, slice(0, 128), 512, ONES, 1.0 / 128, sqp, rsp, nb, o)
                    rs, rst = o[0]
                    kb.op("dve", lambda e: e.scalar_tensor_tensor(out=CKV[:, :], in0=pc[:, :], scalar=gn(C_GCKV), in1=rs[:, :], op0=ALU.mult, op1=ALU.mult),
                          reads=[pct, rst, CTt], writes=[CKVt])

                def ch_kr():
                    o = []
                    yield from rstd_gen(pr_, prt, slice(0, 96), 512, KR32, 1.0 / 32, sqp, rsp, nb, o)
                    rs, rst = o[0]
                    kb.op("dve", lambda e: e.scalar_tensor_tensor(out=KRN[R, :], in0=pr_[R, :], scalar=gn(C_GKR, R), in1=rs[R, :], op0=ALU.mult, op1=ALU.mult),
                          reads=[prt, rst, CTt], writes=[krnt])
                    kb.op("dve", lambda e: e.scalar_tensor_tensor(out=KRS[R, :], in0=pw[R, :], scalar=gn(C_GKRS, R), in1=rs[R, :], op0=ALU.mult, op1=ALU.mult),
                          reads=[pwt, rst, CTt], writes=[krst])
                    yield
                    kb.op("pool", lambda e: e.tensor_tensor(out=KRN[R, :], in0=KRN[R, :], in1=CO[R, :], op=ALU.mult), reads=[krnt, tt["co"]], writes=[krnt])
                    kb.op("pool", lambda e: e.tensor_tensor(out=KRS[R, :], in0=KRS[R, :], in1=SN[R, :], op=ALU.mult), reads=[krst, tt["sn"]], writes=[krst])
                    yield
                    for i in range(2):
                        kb.op("pool", lambda e: e.tensor_tensor(out=Kbuf[i][R, ts], in0=KRN[R, :], in1=KRS[R, :], op=ALU.add), reads=[krnt, krst], writes=[Kbt_r[i]])
                wave([ch_ckv(), ch_kr()])
                yield

                def ch_kn(i):
                    pk, pkt = nb()
                    kb.op("pe", lambda e: e.matmul(pk[:, :], lhsT=WUK[:, i * 128:(i + 1) * 128], rhs=CKV[:, :], start=True, stop=True), reads=[WUKt, CKVt], writes=[pkt])
                    yield
                    o = []
                    yield from rstd_gen(pk, pkt, slice(0, 128), 512, BLK64, 1.0 / 64, sqp, rsp, nb, o)
                    rs, rst = o[0]
                    KN, KNt = KNp.next()
                    kb.op("dve", lambda e: e.scalar_tensor_tensor(out=KN[:, :], in0=pk[:, :], scalar=gn(C_GBK), in1=rs[:, :], op0=ALU.mult, op1=ALU.mult),
                          reads=[pkt, rst, CTt], writes=[KNt])
                    yield
                    kb.dma("sp", Kd[i * 128:(i + 1) * 128, ts], KN[:, :], reads=[KNt])
                for w in range(2):
                    wave([ch_kn(i) for i in range(4 * w, 4 * w + 4)])
                    yield
                VS, VSt = VST[bt % 2], VSTt[bt % 2]

                def ch_v(blk, half):
                    pv_, pvt = nb()
                    kb.op("pe", lambda e: e.matmul(pv_[:, :], lhsT=CKV[:, blk * 128:(blk + 1) * 128], rhs=WUV[:, half * 512:(half + 1) * 512], start=True, stop=True),
                          reads=[WUVt, CKVt], writes=[pvt])
                    yield
                    p4 = pv_[:, :].rearrange("p (a b e) -> p a b e", a=4, b=2)
                    for par in range(2):
                        kb.op("dve", lambda e: e.tensor_copy(out=VS[:, half * 8 + par:half * 8 + 8:2, blk, par * 64:par * 64 + 64], in_=p4[:, :, par, :]),
                              reads=[pvt], writes=[VSt])
                        yield
                for w in range(2):
                    wave([ch_v(blk, half) for blk in (2 * w, 2 * w + 1) for half in range(2)])
                    yield
                kb.dma("sp", Vd4[:, :, bt, :], VS[:, :, :, :].rearrange("p h b e -> p h (b e)"), reads=[VSt])
            run_pipe([(lambda bt=bt: body_c(bt)) for bt in range(16)], 3)
            kb.barrier()

        for ph in _phase("3"):
            def s2(name, shape, dt):
                return sb("d_" + name, shape, dt, ph)
            Vbuf = [s2("Vbuf%d" % i, [128, 64, 128], BF16) for i in range(2)]; Vbt = [Trk(), Trk()]
            CM = s2("CM", [128, 8, 512], BF16); CMt = Trk()
            kb.dma("pool", CM[:, :, :], cmk.rearrange("p (r q) -> p r q", r=8), writes=[CMt])
            Qt = RR([(s2("Qt%d" % i, [96, 512], BF16), Trk()) for i in range(3)])
            Pp = RR([(s2("P%d" % i, [128, 2, 512], BF16), Trk()) for i in range(3)])
            DENp = RR([(s2("DEN%d" % i, [128, 512], F32), Trk()) for i in range(2)])
            OBp = RR([(s2("OB%d" % i, [128, 512], F32), Trk()) for i in range(2)])
            SPp = RR([(pst("d_S%d" % i, [128, 2, 512], ph), Trk()) for i in range(3)])
            OPp = RR([(pst("d_O%d" % i, [128, 512], ph), Trk()) for i in range(2)])
            Kd3 = Kd.rearrange("(h r) t -> h r t", r=64)

            def loadkv(h):
                b = h % 2
                kb.dma("sp", Kbuf[b][0:64, :], Kd3[h], writes=[Kbt_n[b]])
                kb.dma("sp", Vbuf[b][:, :, :].rearrange("p (b k) e -> p b (k e)", k=4), Vd[h].rearrange("b t e -> t b e"), writes=[Vbt[b]])
            SC = 96.0 ** -0.5
            iters = [(h, j) for h in range(16) for j in range(8)]
            groups = [(i, gi) for i, (h, j) in enumerate(iters) for gi in range(4 * j + 4)]
            T = len(groups)
            Qs = {}; Os = {}; Ss = {}
            Qs[0] = Qt.next()
            kb.dma("sp", Qs[0][0][:, :], Qd[0, :, 0:512], writes=[Qs[0][1]])
            loadkv(0)

            def qk(t):
                i, gi = groups[t]
                h, j = iters[i]
                b = h % 2
                if gi == 0 and i + 1 < len(iters):
                    Qs[i + 1] = Qt.next()
                    h2, j2 = iters[i + 1]
                    kb.dma("sp", Qs[i + 1][0][:, :], Qd[h2, :, j2 * 512:(j2 + 1) * 512], writes=[Qs[i + 1][1]])
                Q, Qtt = Qs[i]
                S, St = SPp.next()
                ngq = 4 * j + 4
                qlo = (0, 0, 256, 256)[max(0, gi - (ngq - 4))]
                for u in range(2):
                    kbi = gi * 2 + u
                    kb.op("pe", lambda e: e.matmul(S[:, u, qlo:512], lhsT=Kbuf[b][0:96, kbi * 128:(kbi + 1) * 128], rhs=Q[0:96, qlo:512], start=True, stop=True),
                          reads=[Kbt_n[b], Kbt_r[b], Qtt], writes=[St], sig=(u == 1))
                Ss[t] = (S, St)

            def rest(t):
                i, gi = groups[t]
                h, j = iters[i]
                b = h % 2; par = h % 2
                ng = 4 * j + 4
                osl = slice(par * 64, par * 64 + 64); ssl = slice((1 - par) * 64, (1 - par) * 64 + 64)
                if gi == 0:
                    Os[i] = OPp.next()
                    if j == 0 and h + 1 < 16:
                        loadkv(h + 1)
                O, Ot = Os[i]
                S, St = Ss.pop(t)
                P, Pt = Pp.next()
                qlo = (0, 0, 256, 256)[max(0, gi - (ng - 4))]
                kb.op("act", lambda e: e.activation(out=P[:, :, qlo:512], in_=S[:, :, qlo:512], func=AF.Exp, scale=SC), reads=[St], writes=[Pt])
                if gi >= ng - 4:
                    r0 = (gi - (ng - 4)) * 2
                    kb.op("dve", lambda e: e.tensor_tensor(out=P[:, :, qlo:512], in0=P[:, :, qlo:512], in1=CM[:, r0:r0 + 2, qlo:512], op=ALU.mult), reads=[Pt, CMt], writes=[Pt])
                for u in range(2):
                    kbi = gi * 2 + u
                    kb.op("pe", lambda e: e.matmul(O[:, qlo:512], lhsT=Vbuf[b][:, kbi, :], rhs=P[:, u, qlo:512], start=(kbi == 0), stop=(kbi == 2 * ng - 1)),
                          reads=[Vbt[b], Pt], writes=[Ot], sig=(u == 1))
                if gi == ng - 1:
                    DEN, DENt = DENp.next(); OB, OBt = OBp.next()
                    kb.op("dve", lambda e: e.reciprocal(out=DEN[osl, :], in_=O[ssl, :]), reads=[Ot], writes=[DENt])
                    kb.op("dve", lambda e: e.tensor_tensor(out=OB[osl, :], in0=O[osl, :], in1=DEN[osl, :], op=ALU.mult), reads=[Ot, DENt], writes=[OBt])
                    kb.dma("sp", OBd[h * 64:(h + 1) * 64, j * 512:(j + 1) * 512], OB[osl, :], reads=[OBt])
                    del Os[i]
            qk(0); qk(1)
            for t in range(T):
                if t + 2 < T:
                    qk(t + 2)
                rest(t)
            kb.barrier()

        kvs.close()
        for ph in _phase("4"):
            def s2(name, shape, dt):
                return sb("e_" + name, shape, dt, ph)
            NT = 256
            W4 = s2("W4", [128, 8, 2048], BF16); W4t = Trk()
            WOB = s2("WOB", [128, 8, 1024], BF16); WOBt = Trk()
            WOUT = s2("WOUT", [128, 8, 1024], BF16); WOUTt = Trk()
            WPG = s2("WPG", [128, 8, 1024], BF16); WPGt = Trk()
            WPP = s2("WPP", [128, 2, 1024], BF16); WPPt = Trk()
            kb.dma("pool", W4[:, :, :], wview(w4), writes=[W4t])
            kb.dma("pool", WOB[:, :, :], wview(wob), writes=[WOBt])
            kb.dma("pool", WOUT[:, :, :], wview(wout), writes=[WOUTt])
            kb.dma("pool", WPG[:, :, :], wview(wpg), writes=[WPGt])
            kb.dma("pool", WPP[:, :, :], wview(wpp), writes=[WPPt])
            X = [s2("X%d" % i, [128, 8, NT], F32) for i in range(3)]; Xt = [[Trk() for _ in range(8)] for _ in range(3)]; Xall = [Trk(), Trk(), Trk()]
            SQ = s2("SQ", [128, 8, NT], BF16); SQt = Trk()
            RSs = [(s2("RS%d" % i, [128, NT], F32), Trk()) for i in range(2)]
            Hs = [(s2("H%d" % i, [128, 8, NT], BF16), [Trk() for _ in range(8)]) for i in range(2)]
            YBs = [(s2("YB%d" % i, [128, 8, NT], BF16), [Trk() for _ in range(8)]) for i in range(2)]
            MGs = [(s2("MG%d" % i, [128, 8, NT], BF16), [Trk() for _ in range(8)]) for i in range(2)]
            EMs = [(s2("EM%d" % i, [128, 8, NT], F32), [Trk() for _ in range(8)]) for i in range(2)]
            GTs = [(s2("GT%d" % i, [128, 8, NT], F32), [Trk() for _ in range(8)]) for i in range(2)]
            PBs = [(s2("PB%d" % i, [128, 2, NT], BF16), Trk()) for i in range(2)]
            PF = [s2("PF%d" % i, [128, 2, NT], F32) for i in range(3)]; PFt = [Trk(), Trk(), Trk()]
            OBl = RR([(s2("OBl%d" % i, [128, NT], F32), Trk()) for i in range(4)])
            MAl = RR([(s2("MAl%d" % i, [128, NT], F32), Trk()) for i in range(4)])
            SZp = RR([(s2("SZ%d" % i, [128, NT], F32), Trk()) for i in range(4)])
            SGp = RR([(s2("SG%d" % i, [128, NT], F32), Trk()) for i in range(4)])
            T1p = RR([(s2("T1%d" % i, [128, NT], F32), Trk()) for i in range(4)])
            PSg = RR([(pst("e_ps%d" % i, [128, 512], ph), Trk()) for i in range(8)])
            nb = PSg.next
            xs4 = xsT.rearrange("(c p) (k t) -> p c k t", p=128, t=384)
            p3 = pT.rearrange("(c p) t -> p c t", p=128)

            def loadx(ti, Xd, Xdt, Xa):
                kb.dma("act", Xd[:, :, :], xs4[:, :, ti, 128:384], writes=Xdt + [Xa])
            def body_e(ti):
                ts = slice(ti * NT, (ti + 1) * NT)
                Xc, Xct, Xa = X[ti % 3], Xt[ti % 3], Xall[ti % 3]
                PFc, PFct = PF[ti % 3], PFt[ti % 3]
                RS, RSt = RSs[ti % 2]; H, Ht = Hs[ti % 2]; YB, YBt = YBs[ti % 2]; MG, MGt = MGs[ti % 2]
                EM, EMt = EMs[ti % 2]; GT, GTt = GTs[ti % 2]; PB, PBt = PBs[ti % 2]
                if ti == 0:
                    loadx(0, X[0], Xt[0], Xall[0])
                    kb.dma("act", PF[0][:, :, :], p3[:, :, 0:NT], writes=[PFt[0]])
                if ti + 1 < 16:
                    n3 = (ti + 1) % 3
                    loadx(ti + 1, X[n3], Xt[n3], Xall[n3])
                    kb.dma("act", PF[n3][:, :, :], p3[:, :, (ti + 1) * NT:(ti + 2) * NT], writes=[PFt[n3]])
                norm1024(Xc, Xa, H, Ht, NT, SQ, SQt, RS, RSt, C_GN, nb)
                yield
                def ch_zb(c):
                    ob, obt = OBl.next()
                    kb.dma("pool", ob[:, :], OBd[c * 128:(c + 1) * 128, ts], writes=[obt])
                    ps, pt = nb()
                    for k in range(8):
                        kb.op("pe", lambda e: e.matmul(ps[:, 0:NT], lhsT=W4[:, k, c * 128:(c + 1) * 128], rhs=H[:, k, :], start=(k == 0), stop=(k == 7)),
                              reads=[W4t, Ht[k]], writes=[pt], sig=(k == 7))
                    yield
                    sz, szt = SZp.next()
                    kb.op("act", lambda e: e.activation(out=sz[:, :], in_=ps[:, 0:NT], func=AF.Silu), reads=[pt], writes=[szt])
                    yield
                    kb.op("dve", lambda e: e.tensor_tensor(out=YB[:, c, :], in0=sz[:, :], in1=ob[:, :], op=ALU.mult), reads=[szt, obt], writes=[YBt[c]])
                for w in range(2):
                    wave([ch_zb(c) for c in range(4 * w, 4 * w + 4)])
                yield

                def ch_mg(oc):
                    ma, mat = MAl.next()
                    kb.dma("pool", ma[:, :], MAd[oc * 128:(oc + 1) * 128, ts], writes=[mat])
                    py, pyt = nb(); pg, pgt = nb()
                    for k in range(8):
                        kb.op("pe", lambda e: e.matmul(py[:, 0:NT], lhsT=WOB[:, k, oc * 128:(oc + 1) * 128], rhs=YB[:, k, :], start=(k == 0), stop=(k == 7)),
                              reads=[WOBt, YBt[k]], writes=[pyt], sig=(k == 7))
                    yield
                    for k in range(8):
                        kb.op("pe", lambda e: e.matmul(pg[:, 0:NT], lhsT=W4[:, k, 1024 + oc * 128:1152 + oc * 128], rhs=H[:, k, :], start=(k == 0), stop=(k == 7)),
                              reads=[W4t, Ht[k]], writes=[pgt], sig=(k == 7))
                    yield
                    sg, sgt = SGp.next()
                    kb.op("act", lambda e: e.activation(out=sg[:, :], in_=pg[:, 0:NT], func=AF.Sigmoid), reads=[pgt], writes=[sgt])
                    yield
                    t1, t1t = T1p.next()
                    kb.op("dve", lambda e: e.tensor_tensor(out=t1[:, :], in0=py[:, 0:NT], in1=sg[:, :], op=ALU.mult), reads=[pyt, sgt], writes=[t1t])
                    yield
                    kb.op("dve", lambda e: e.tensor_tensor(out=MG[:, oc, :], in0=t1[:, :], in1=ma[:, :], op=ALU.add), reads=[t1t, mat], writes=[MGt[oc]])
                for w in range(2):
                    wave([ch_mg(oc) for oc in range(4 * w, 4 * w + 4)])
                yield

                def ch_x(oc):
                    px, pxt = nb()
                    for k in range(8):
                        kb.op("pe", lambda e: e.matmul(px[:, 0:NT], lhsT=WOUT[:, k, oc * 128:(oc + 1) * 128], rhs=MG[:, k, :], start=(k == 0), stop=(k == 7)),
                              reads=[WOUTt, MGt[k]], writes=[pxt], sig=(k == 7))
                    yield
                    kb.op("dve", lambda e: e.tensor_tensor(out=Xc[:, oc, :], in0=px[:, 0:NT], in1=Xc[:, oc, :], op=ALU.add), reads=[pxt, Xct[oc], Xa], writes=[Xct[oc]])
                for w in range(2):
                    wave([ch_x(oc) for oc in range(4 * w, 4 * w + 4)])
                    yield
                kb.op("act", lambda e: e.activation(out=PB[:, :, :], in_=PFc[:, :, :], func=AF.Copy), reads=[PFct], writes=[PBt])

                def ch_e(oc):
                    pe_, pet = nb()
                    for k in range(2):
                        kb.op("pe", lambda e: e.matmul(pe_[:, 0:NT], lhsT=WPP[:, k, oc * 128:(oc + 1) * 128], rhs=PB[:, k, :], start=(k == 0), stop=(k == 1)),
                              reads=[WPPt, PBt], writes=[pet], sig=(k == 1))
                    yield
                    kb.op("dve", lambda e: e.tensor_copy(out=EM[:, oc, :], in_=pe_[:, 0:NT]), reads=[pet], writes=[EMt[oc]])
                for w in range(2):
                    wave([ch_e(oc) for oc in range(4 * w, 4 * w + 4)])
                    yield
                kb.op("act", lambda e: e.activation(out=SQ[:, :, :], in_=Xc[:, :, :], func=AF.Square), reads=Xct + [Xa], writes=[SQt])
                ps, pt = nb()
                for c in range(8):
                    kb.op("pe", lambda e: e.matmul(ps[:, 0:NT], lhsT=ONES, rhs=SQ[:, c, :], start=(c == 0), stop=(c == 7)), reads=[SQt, CMATt], writes=[pt], sig=(c == 7))
                kb.op("act", lambda e: e.activation(out=RS[:, :], in_=ps[:, 0:NT], func=AF.Ln, scale=1.0 / 1024, bias=gn(C_EPS)), reads=[pt, CTt], writes=[RSt])
                kb.op("act", lambda e: e.activation(out=RS[:, :], in_=RS[:, :], func=AF.Exp, scale=-0.5), reads=[RSt], writes=[RSt])
                for c in range(8):
                    kb.op("dve", lambda e: e.scalar_tensor_tensor(out=H[:, c, :], in0=Xc[:, c, :], scalar=gn(C_GPLE + c), in1=RS[:, :], op0=ALU.mult, op1=ALU.mult),
                          reads=[Xct[c], Xa, RSt, CTt], writes=[Ht[c]])
                yield
                def ch_g(oc):
                    pg, pgt = nb()
                    for k in range(8):
                        kb.op("pe", lambda e: e.matmul(pg[:, 0:NT], lhsT=WPG[:, k, oc * 128:(oc + 1) * 128], rhs=H[:, k, :], start=(k == 0), stop=(k == 7)),
                              reads=[WPGt, Ht[k]], writes=[pgt], sig=(k == 7))
                    yield
                    kb.op("act", lambda e: e.activation(out=GT[:, oc, :], in_=pg[:, 0:NT], func=AF.Sigmoid), reads=[pgt], writes=[GTt[oc]])
                for w in range(2):
                    wave([ch_g(oc) for oc in range(4 * w, 4 * w + 4)])
                yield
                kb.op("act", lambda e: e.activation(out=SQ[:, :, :], in_=EM[:, :, :], func=AF.Square), reads=EMt, writes=[SQt])
                ps, pt = nb()
                for c in range(8):
                    kb.op("pe", lambda e: e.matmul(ps[:, 0:NT], lhsT=ONES, rhs=SQ[:, c, :], start=(c == 0), stop=(c == 7)), reads=[SQt, CMATt], writes=[pt], sig=(c == 7))
                kb.op("act", lambda e: e.activation(out=RS[:, :], in_=ps[:, 0:NT], func=AF.Ln, scale=1.0 / 1024, bias=gn(C_EPS)), reads=[pt, CTt], writes=[RSt])
                kb.op("act", lambda e: e.activation(out=RS[:, :], in_=RS[:, :], func=AF.Exp, scale=-0.5), reads=[RSt], writes=[RSt])
                for oc in range(8):
                    kb.op("dve", lambda e: e.scalar_tensor_tensor(out=EM[:, oc, :], in0=EM[:, oc, :], scalar=gn(C_GPOST + oc), in1=RS[:, :], op0=ALU.mult, op1=ALU.mult),
                          reads=[EMt[oc], RSt, CTt], writes=[EMt[oc]])
                    kb.op("dve", lambda e: e.tensor_tensor(out=EM[:, oc, :], in0=EM[:, oc, :], in1=GT[:, oc, :], op=ALU.mult), reads=[EMt[oc], GTt[oc]], writes=[EMt[oc]])
                    kb.op("dve", lambda e: e.tensor_tensor(out=EM[:, oc, :], in0=EM[:, oc, :], in1=Xc[:, oc, :], op=ALU.add), reads=[EMt[oc], Xct[oc], Xa], writes=[EMt[oc]])
                yield
                kb.dma("sp", outT.rearrange("(c p) t -> p c t", p=128)[:, :, ts], EM[:, :, :], reads=EMt)
            run_pipe([(lambda ti=ti: body_e(ti)) for ti in range(16)], 5)
            kb.barrier()
        kb.finish()
    return nc


def _t5_bucket(d):
    d = np.asarray(d)
    dd = np.maximum(d, 1).astype(np.float32)
    large = 16 + (np.log(dd / 16) / math.log(128 / 16) * 16).astype(np.int32)
    large = np.minimum(large, 31)
    return np.where(d < 16, d, large)


def _own_chunks(par):
    res = []
    for j in range(8):
        res += [4 * j + (0 if par == 0 else 1), 4 * j + (3 if par == 0 else 2)]
    return res


_NC_CACHE = {}


def kernel(x, p, positions, norm_g, w_in, a_q_norm, a_k_norm, a_sinks, rel_bias, w_o_a,
           b_cq_norm, w_uq, b_ckv_norm, w_uk, w_uv, b_q_norm, b_k_norm, b_kr_norm, w_o_b,
           w_out, ple_norm_g, w_ple_gate, w_ple_proj, ple_post_g, _debug=False):
    f32 = np.float32
    x = np.asarray(x, f32); p = np.asarray(p, f32); positions = np.asarray(positions, np.int32)
    W = np.asarray(w_in, f32)[0]
    qa, ka, va, za = W[:, 0:1024], W[:, 1024:1152], W[:, 1152:1280], W[:, 1280:2304]
    cq, ckv, kr, zb = W[:, 2304:2560], W[:, 2560:2688], W[:, 2688:2720], W[:, 2720:3744]
    ga, gb = W[:, 3744:4768], W[:, 4768:5792]
    sw = np.concatenate([np.arange(16, 32), np.arange(0, 16)])
    kadup = np.concatenate([ka[:, 0:64], ka[:, 0:64], ka[:, 64:128], ka[:, 64:128]], axis=1)
    w2a = np.ascontiguousarray(np.concatenate([qa, kadup, va, za, ga], axis=1))
    z64 = np.zeros((1024, 64), f32)
    w1 = np.ascontiguousarray(np.concatenate([ckv, z64, kr, z64, kr[:, sw]], axis=1))
    w4 = np.ascontiguousarray(np.concatenate([zb, gb], axis=1))
    wuq_ = np.asarray(w_uq, f32)[0]
    wuq3 = wuq_.reshape(256, 16, 96)
    wuqs_ = np.ascontiguousarray(np.concatenate([wuq3[:, :, 0:64], wuq3[:, :, 64:96][:, :, sw]], axis=2).reshape(256, 1536))

    def col8(g):
        return np.asarray(g, f32).reshape(8, 128).T

    cst = np.zeros((128, 64), f32)
    cst[:, C_GN:C_GN + 8] = col8(norm_g[0]); cst[:, C_GPLE:C_GPLE + 8] = col8(ple_norm_g[0]); cst[:, C_GPOST:C_GPOST + 8] = col8(ple_post_g[0])
    cst[:, C_GAQ] = np.tile(np.asarray(a_q_norm, f32)[0], 2); cst[:, C_GAK] = np.tile(np.asarray(a_k_norm, f32)[0], 2)
    cst[:, C_GCQ:C_GCQ + 2] = np.asarray(b_cq_norm, f32)[0].reshape(2, 128).T
    cst[:, C_GCKV] = np.asarray(b_ckv_norm, f32)[0]
    bq = np.asarray(b_q_norm, f32)[0]
    cst[0:96, C_GBQ] = bq
    cst[0:96, C_GBQS] = np.concatenate([bq[0:64], bq[64:96][sw]])
    cst[:, C_GBK] = np.tile(np.asarray(b_k_norm, f32)[0], 2)
    bkr = np.asarray(b_kr_norm, f32)[0]
    cst[64:96, C_GKR] = bkr; cst[64:96, C_GKRS] = bkr[sw]
    cst[:, C_SINK:C_SINK + 16] = np.broadcast_to(np.asarray(a_sinks, f32)[0], (128, 16))
    inv_freq = (np.float32(10000.0) ** (-np.arange(0, 32, 2, dtype=np.float32) / np.float32(32))).astype(f32)
    cst[64:96, C_INVF] = np.tile(inv_freq, 2)
    cst[64:80, C_SSGN] = -1.0; cst[80:96, C_SSGN] = 1.0
    cst[:, C_EPS] = EPS; cst[:, C_HPI] = math.pi / 2
    cmat = np.zeros((128, 640), f32)
    cmat[np.arange(128), 512 + 127 - np.arange(128)] = 1.0
    cmat[:, 0:128] = 1.0
    cmat[0:64, 128:192] = 1.0; cmat[64:128, 192:256] = 1.0
    cmat[0:96, 256:352] = 1.0
    cmat[64:96, 384 + 64:384 + 96] = 1.0
    bsel = np.zeros((32, 128), f32)
    bsel[_t5_bucket(np.arange(128)), np.arange(128)] = 1.0
    hperm = [hg * 4 + 2 * i + par for hg in range(4) for par in range(2) for i in range(2)]
    relb = np.ascontiguousarray(np.asarray(rel_bias, f32)[:, hperm])

    in_maps = []
    metas = []
    for c in range(8):
        b, par = c // 2, c % 2
        chunks = _own_chunks(par)
        xb = x[b]
        xbT = np.ascontiguousarray(xb.T)
        segs = []
        hfl = np.ones((16, 128, 128), f32)
        own_idx = []
        for i, ch in enumerate(chunks):
            if ch == 0:
                segs.append(np.zeros((128, 1024), f32)); hfl[i] = 0.0
            else:
                segs.append(xb[ch * 256 - 128:ch * 256])
            segs.append(xb[ch * 256:(ch + 1) * 256])
            own_idx.append(np.arange(ch * 256, (ch + 1) * 256))
        own_idx = np.concatenate(own_idx)
        xsT = np.ascontiguousarray(np.concatenate(segs, axis=0).T)
        pT = np.ascontiguousarray(p[0, b][own_idx].T)
        posb = np.ascontiguousarray(np.broadcast_to(positions[b][None, :], (32, 8192))).astype(np.int32)
        poso = np.ascontiguousarray(np.broadcast_to(positions[b][own_idx][None, :], (32, 4096))).astype(np.int32)
        offs = [0, 1, 6, 7] if par == 0 else [2, 3, 4, 5]
        cm = np.zeros((128, 8, 512), f32)
        kk = np.arange(128)[:, None]; qi = np.arange(128)[None, :]
        for r in range(8):
            for qb in range(4):
                if r < offs[qb]:
                    cm[:, r, qb * 128:(qb + 1) * 128] = 1.0
                elif r == offs[qb]:
                    cm[:, r, qb * 128:(qb + 1) * 128] = (kk <= qi).astype(f32)
        in_maps.append(dict(
            xbT=xbT, xsT=xsT, pT=pT, posb=posb, poso=poso, hfl=hfl, cst=cst, cmat=cmat, bsel=bsel, relb=relb,
            cmk=np.ascontiguousarray(cm.reshape(128, 4096)),
            w2a=w2a, woa=np.ascontiguousarray(np.asarray(w_o_a, f32)[0]), w2b=np.ascontiguousarray(cq),
            wuq=np.ascontiguousarray(wuq_), wuqs=wuqs_, w1=w1,
            wuk=np.ascontiguousarray(np.asarray(w_uk, f32)[0]), wuv=np.ascontiguousarray(np.asarray(w_uv, f32)[0]),
            w4=w4, wob=np.ascontiguousarray(np.asarray(w_o_b, f32)[0]), wout=np.ascontiguousarray(np.asarray(w_out, f32)[0]),
            wpg=np.ascontiguousarray(np.asarray(w_ple_gate, f32)[0]), wpp=np.ascontiguousarray(np.asarray(w_ple_proj, f32)[0]),
        ))
        metas.append((b, own_idx))
    key = bool(_debug)
    if key not in _NC_CACHE:
        _NC_CACHE[key] = build(debug=key)
    nc = _NC_CACHE[key]
    res = run_bass_kernel_spmd(nc, in_maps, core_ids=list(range(8)))
    out = np.empty((4, 8192, 1024), f32)
    for c in range(8):
        b, own_idx = metas[c]
        out[b, own_idx, :] = np.asarray(res.results[c]["outT"]).T
    if _debug:
        return out, res.results, metas
    return out
```
